# Optimizing a Trainium2 kernel written in Bass

```python
import jax, jax.numpy as jnp
from jax import lax
import numpy as np

D_MODEL = 2048
BATCH = 2
SEQ = 4096
DEPTH = 2
DEC_BATCH = 8
DEC_SEQ = 16
PAST_LEN = 2048

CHUNK = 64
N_PAST_CHUNKS = 8
PAST_ROWS = N_PAST_CHUNKS * CHUNK
BAND = (N_PAST_CHUNKS + 1) * CHUNK
N_A_LAYERS = DEPTH // 2
N_B_LAYERS = DEPTH - N_A_LAYERS
CONV_WIDTH = 3
HEAD_DIM = 128
N_HEADS = D_MODEL // HEAD_DIM
D_ATTN = N_HEADS * HEAD_DIM
D_FF = ((8 * D_MODEL // 3 + 255) // 256) * 256
MAX_REL = 128
N_REL = 2 * MAX_REL + 1
EPS = 1e-6
FFN_RES = 0.5
NEG_INF = -1e30

kernel_name = 'yoco_shortconv_chunkband_streaming_step'


def rms_norm(x, g):
    xf = x.astype(jnp.float32)
    y = xf * lax.rsqrt(jnp.mean(xf * xf, axis=-1, keepdims=True) + EPS)
    return (y * g.astype(jnp.float32)).astype(x.dtype)


def swiglu(x, wg, wu, wd):
    return (jax.nn.silu(x @ wg) * (x @ wu)) @ wd


def short_conv_mixer(xn, w_in, conv_w, w_out, prev):
    b, c, xb = jnp.split(xn @ w_in, 3, axis=-1)
    u = c * xb
    up = jnp.concatenate([prev.astype(u.dtype), u], axis=1)
    t = u.shape[1]
    conv = conv_w[0] * up[:, 0:t]
    for tap in range(1, CONV_WIDTH):
        conv = conv + conv_w[tap] * up[:, tap:tap + t]
    return (b * conv) @ w_out, up[:, -(CONV_WIDTH - 1):]


def rel_bias_matrix(rel_bias, offset, tq, tk):
    dist = jnp.arange(tq)[:, None] - jnp.arange(tk)[None, :] + offset
    idx = jnp.clip(dist, -MAX_REL, MAX_REL) + MAX_REL
    return rel_bias[:, idx]


def attend(q, k, v, bias, valid=None):
    s = jnp.einsum('bqhd,bkhd->bhqk', q, k).astype(jnp.float32) * (HEAD_DIM ** -0.5)
    s = s + bias.astype(jnp.float32)
    if valid is not None:
        s = jnp.where(valid, s, NEG_INF)
    p = jax.nn.softmax(s, axis=-1).astype(v.dtype)
    return jnp.einsum('bhqk,bkhd->bqhd', p, v)


def chunk_band_attention_prompt(q, k, v, rel_bias):
    bsz, s, h, dh = q.shape
    n_chunks = s // CHUNK
    pad = ((0, 0), (PAST_ROWS, 0), (0, 0), (0, 0))
    kp = jnp.pad(k, pad)
    vp = jnp.pad(v, pad)
    bias = rel_bias_matrix(rel_bias, PAST_ROWS, CHUNK, BAND)

    def one_chunk(c):
        start = c * CHUNK
        qc = lax.dynamic_slice_in_dim(q, start, CHUNK, axis=1)
        kc = lax.dynamic_slice_in_dim(kp, start, BAND, axis=1)
        vc = lax.dynamic_slice_in_dim(vp, start, BAND, axis=1)
        valid = (start - PAST_ROWS + jnp.arange(BAND)) >= 0
        return attend(qc, kc, vc, bias, valid)

    out = lax.map(one_chunk, jnp.arange(n_chunks))
    return jnp.moveaxis(out, 0, 1).reshape(bsz, s, h * dh)


def chunk_band_attention_sample(q, k_new, v_new, cache_k, cache_v, rel_bias):
    bsz, t, h, dh = q.shape
    cl = cache_k.shape[1]
    k = jnp.concatenate([cache_k.astype(k_new.dtype), k_new], axis=1)
    v = jnp.concatenate([cache_v.astype(v_new.dtype), v_new], axis=1)
    bias = rel_bias_matrix(rel_bias, cl, t, cl + t)
    return attend(q, k, v, bias).reshape(bsz, t, h * dh)


def trunk(x, conv_prev, cache_k, cache_v, ffn_norm, ffn_w_gate, ffn_w_up, ffn_w_down,
          mix_norm, conv_w_in, conv_w, conv_w_out, kv_norm, w_kv, k_gain,
          w_q, q_gain, rel_bias, w_o):
    bsz, t, _ = x.shape
    h = x
    new_conv = []
    k_sh = None
    v_sh = None
    for layer in range(DEPTH):
        h = h + FFN_RES * swiglu(rms_norm(h, ffn_norm[layer, 0]), ffn_w_gate[layer, 0],
                                 ffn_w_up[layer, 0], ffn_w_down[layer, 0])
        hn = rms_norm(h, mix_norm[layer])
        if layer < N_A_LAYERS:
            y, st = short_conv_mixer(hn, conv_w_in[layer], conv_w[layer], conv_w_out[layer],
                                     conv_prev[layer])
            new_conv.append(st)
        else:
            j = layer - N_A_LAYERS
            q = rms_norm((hn @ w_q[j]).reshape(bsz, t, N_HEADS, HEAD_DIM), q_gain[j])
            if cache_k is None:
                att = chunk_band_attention_prompt(q, k_sh, v_sh, rel_bias[j])
            else:
                att = chunk_band_attention_sample(q, k_sh, v_sh, cache_k, cache_v, rel_bias[j])
            y = att @ w_o[j]
        h = h + y
        h = h + FFN_RES * swiglu(rms_norm(h, ffn_norm[layer, 1]), ffn_w_gate[layer, 1],
                                 ffn_w_up[layer, 1], ffn_w_down[layer, 1])
        if layer == N_A_LAYERS - 1:
            kv = rms_norm(h, kv_norm) @ w_kv
            k_flat, v_flat = jnp.split(kv, 2, axis=-1)
            k_sh = rms_norm(k_flat.reshape(bsz, t, N_HEADS, HEAD_DIM), k_gain)
            v_sh = v_flat.reshape(bsz, t, N_HEADS, HEAD_DIM)
    return h, jnp.stack(new_conv), k_sh, v_sh


def setup_inputs(seed: int = 0) -> dict:
    key = jax.random.key(seed)
    ks = jax.random.split(key, 24)
    f32 = jnp.float32

    def nrm(k, shape, scale):
        return scale * jax.random.normal(k, shape, f32)

    kv_rows = min(PAST_ROWS, PAST_LEN)
    return {
        'x_prompt': nrm(ks[0], (BATCH, SEQ, D_MODEL), 1.0),
        'x_sample': nrm(ks[1], (DEC_BATCH, DEC_SEQ, D_MODEL), 1.0),
        'state_conv': nrm(ks[2], (N_A_LAYERS, DEC_BATCH, CONV_WIDTH - 1, D_MODEL), 1.0),
        'cache_k': nrm(ks[3], (DEC_BATCH, kv_rows, N_HEADS, HEAD_DIM), 1.0),
        'cache_v': nrm(ks[4], (DEC_BATCH, kv_rows, N_HEADS, HEAD_DIM), 1.0),
        'ffn_norm': 1.0 + nrm(ks[5], (DEPTH, 2, D_MODEL), 0.02),
        'ffn_w_gate': nrm(ks[6], (DEPTH, 2, D_MODEL, D_FF), D_MODEL ** -0.5),
        'ffn_w_up': nrm(ks[7], (DEPTH, 2, D_MODEL, D_FF), D_MODEL ** -0.5),
        'ffn_w_down': nrm(ks[8], (DEPTH, 2, D_FF, D_MODEL), D_FF ** -0.5),
        'mix_norm': 1.0 + nrm(ks[9], (DEPTH, D_MODEL), 0.02),
        'conv_w_in': nrm(ks[10], (N_A_LAYERS, D_MODEL, 3 * D_MODEL), D_MODEL ** -0.5),
        'conv_w': nrm(ks[11], (N_A_LAYERS, CONV_WIDTH, D_MODEL), CONV_WIDTH ** -0.5),
        'conv_w_out': nrm(ks[12], (N_A_LAYERS, D_MODEL, D_MODEL), D_MODEL ** -0.5),
        'kv_norm': 1.0 + nrm(ks[13], (D_MODEL,), 0.02),
        'w_kv': nrm(ks[14], (D_MODEL, 2 * D_ATTN), D_MODEL ** -0.5),
        'k_gain': 1.0 + nrm(ks[15], (HEAD_DIM,), 0.02),
        'w_q': nrm(ks[16], (N_B_LAYERS, D_MODEL, D_ATTN), D_MODEL ** -0.5),
        'q_gain': 1.0 + nrm(ks[17], (N_B_LAYERS, HEAD_DIM), 0.02),
        'rel_bias': nrm(ks[18], (N_B_LAYERS, N_HEADS, N_REL), 0.2),
        'w_o': nrm(ks[19], (N_B_LAYERS, D_ATTN, D_MODEL), D_ATTN ** -0.5),
    }


def reference(x_prompt, x_sample, state_conv, cache_k, cache_v, ffn_norm, ffn_w_gate, ffn_w_up,
              ffn_w_down, mix_norm, conv_w_in, conv_w, conv_w_out, kv_norm, w_kv, k_gain,
              w_q, q_gain, rel_bias, w_o):
    conv_zero = jnp.zeros((N_A_LAYERS, x_prompt.shape[0], CONV_WIDTH - 1, x_prompt.shape[2]),
                          x_prompt.dtype)
    y_prompt, conv_p, k_p, v_p = trunk(
        x_prompt, conv_zero, None, None, ffn_norm, ffn_w_gate, ffn_w_up, ffn_w_down,
        mix_norm, conv_w_in, conv_w, conv_w_out, kv_norm, w_kv, k_gain,
        w_q, q_gain, rel_bias, w_o)
    y_sample, conv_s, k_s, v_s = trunk(
        x_sample, state_conv, cache_k, cache_v, ffn_norm, ffn_w_gate, ffn_w_up, ffn_w_down,
        mix_norm, conv_w_in, conv_w, conv_w_out, kv_norm, w_kv, k_gain,
        w_q, q_gain, rel_bias, w_o)
    keep = min(PAST_ROWS, x_prompt.shape[1])
    return (y_prompt, y_sample, conv_p, k_p[:, -keep:], v_p[:, -keep:], conv_s, k_s, v_s)
```

```python
import numpy as np
from contextlib import ExitStack
import concourse.bass as bass
import concourse.mybir as mybir
from concourse.bass_utils import run_bass_kernel_spmd

F32 = mybir.dt.float32
BF16 = mybir.dt.bfloat16
AF = mybir.ActivationFunctionType
ALU = mybir.AluOpType

D = 2048
NCH = 16
DFF = 5632
NQ = 4
QCH = 11
TW0 = 786
TW1 = 528
NTOK0 = 1554
EPS = 1e-6
SCALE = 128.0 ** -0.5
SLOT = 8192
NSLOT = 4

GC_FFN = 0
GC_MIX = 64
GC_KV = 96
GC_CONV = 112
GC_KG = 160
GC_QG = 161
NGV = 162


PHASES = []


class Space:
    def __init__(self, name, size, exclusive=False):
        self.name = name
        self.exclusive = exclusive
        self.segs = [[0, size, None, {}]]

    def _split(self, pos):
        for i, s in enumerate(self.segs):
            if s[0] < pos < s[1]:
                self.segs.insert(i + 1, [pos, s[1], s[2], dict(s[3])])
                s[1] = pos
                return

    def access(self, lo, hi, write, tok):
        if self.exclusive:
            write = True
        self._split(lo)
        self._split(hi)
        deps = []
        first = None
        i = 0
        while i < len(self.segs):
            s = self.segs[i]
            if s[1] <= lo:
                i += 1
                continue
            if s[0] >= hi:
                break
            if s[2] is not None:
                deps.append((s[2][0], s[2][1], 'waw' if write else 'raw'))
            if write:
                for k, v in s[3].items():
                    deps.append((k, v, 'war'))
                if first is None:
                    first = i
                    s[2] = tok
                    s[3] = {}
                    i += 1
                else:
                    self.segs[first][1] = s[1]
                    del self.segs[i]
            else:
                k, v = tok
                if s[3].get(k, 0) < v:
                    s[3][k] = v
                i += 1
        return deps


class Sched:
    def __init__(self, nc, es):
        self.nc = nc
        self.es = es
        self.eng = {'pe': nc.tensor, 'act': nc.scalar, 'dve': nc.vector,
                    'pool': nc.gpsimd, 'sp': nc.sync}
        self.sem = {}
        for k in ('pe', 'act', 'dve'):
            self.sem[k] = es.enter_context(nc.semaphore('s_' + k))
        self.cnt = {k: 0 for k in self.eng}
        self.waited = {k: {} for k in self.eng}
        self.chan = {}
        self.nwait = 0

    def channel(self, name):
        if name not in self.chan:
            self.chan[name] = [self.es.enter_context(self.nc.semaphore('c_' + name)), 0]
        return name

    def _collect(self, eng, tok, reads, writes):
        deps = []
        for (sp, lo, hi) in reads:
            deps += sp.access(lo, hi, False, tok)
        for (sp, lo, hi) in writes:
            deps += sp.access(lo, hi, True, tok)
        best = {}
        for (k, v, kind) in deps:
            if k == eng:
                if eng == 'pe':
                    continue
                if v >= tok[1]:
                    continue
            if best.get(k, 0) < v:
                best[k] = v
        for k, v in best.items():
            if self.waited[eng].get(k, 0) >= v:
                continue
            self.waited[eng][k] = v
            sem = self.sem[k] if k in self.sem else self.chan[k][0]
            self.eng[eng].wait_ge(sem, v)
            self.nwait += 1

    def op(self, eng, fn, reads=(), writes=()):
        tok = (eng, self.cnt[eng] + 1)
        self._collect(eng, tok, reads, writes)
        inst = fn(self.eng[eng])
        inst.then_inc(self.sem[eng], 1)
        self.cnt[eng] += 1

    def dma(self, q, chname, out_ap, in_ap, reads=(), writes=()):
        ch = self.chan[self.channel(chname)]
        tok = (chname, 16 * (ch[1] + 1))
        self._collect(q, tok, reads, writes)
        self.eng[q].dma_start(out=out_ap, in_=in_ap).then_inc(ch[0], 16)
        ch[1] += 1

    def final_wait(self, eng='sp'):
        for name, (sem, cnt) in self.chan.items():
            if cnt > 0:
                self.eng[eng].wait_ge(sem, 16 * cnt)
        for k in ('pe', 'act', 'dve'):
            if self.cnt[k] > 0:
                self.eng[eng].wait_ge(self.sem[k], self.cnt[k])


class Buf:
    def __init__(self, nc, es, name, n, dt):
        self.t = es.enter_context(nc.sbuf_tensor(name, [128, n], dt))
        self.sp = Space(name, n)
        self.n = n

    def ap(self, lo, hi, p0=0, p1=128):
        return self.t[p0:p1, lo:hi]

    def rg(self, lo, hi):
        return (self.sp, lo, hi)

    def sl(self, lo, hi):
        return self.t[:, lo:hi]


class View:
    def __init__(self, parent, base, n):
        self.parent = parent
        self.base = base
        self.n = n
        self.sp = parent.sp

    def ap(self, lo, hi, p0=0, p1=128):
        return self.parent.ap(self.base + lo, self.base + hi, p0, p1)

    def rg(self, lo, hi):
        return self.parent.rg(self.base + lo, self.base + hi)

    def sl(self, lo, hi):
        return self.parent.t[:, self.base + lo:self.base + hi]


def build_nc():
    nc = bass.Bass("TRN2", target_bir_lowering=False)

    def din(name, shape, dt=F32):
        return nc.dram_tensor(name, list(shape), dt, kind="ExternalInput").ap()

    def dout(name, shape, dt=F32):
        return nc.dram_tensor(name, list(shape), dt, kind="ExternalOutput").ap()

    xT = din("xT", [D, NTOK0])
    stT = din("stT", [128, 32])
    kcT = din("kcT", [16, 128, 512])
    vc = din("vc", [512, D])
    w_gate = din("w_gate", [4, D, DFF])
    w_up = din("w_up", [4, D, DFF])
    w_down = din("w_down", [4, DFF, D])
    w_in = din("w_in", [D, 3 * D])
    w_out = din("w_out", [D, D])
    w_kv = din("w_kv", [D, 2 * D])
    w_q = din("w_q", [D, D])
    w_o = din("w_o", [D, D])
    gv_in = din("gv", [128, NGV])
    biasT = din("biasT", [16, 128, 640])
    biasTs = din("biasTs", [16, 128, 80])
    maskT = din("maskT", [128, 640])
    vones_in = din("vones", [128, 12 * 128])

    yT = dout("yT", [D, 1040])
    convp = dout("convp", [128, 32])
    convs = dout("convs", [128, 32])
    kpT = dout("kpT", [D, 512])
    vp = dout("vp", [512, D])
    ksT = dout("ksT", [D, 16])
    vs = dout("vs", [16, D])

    h_scr = nc.dram_tensor("h_scr", [128, NCH * 1040], F32, kind="Internal").ap()
    kt_scr = nc.dram_tensor("kt_scr", [16, 128, 1536], BF16, kind="Internal").ap()
    v_scr = nc.dram_tensor("v_scr", [1536, D], BF16, kind="Internal").ap()
    sp_hscr = Space("h_scr", NCH * 1040)
    sp_kt = Space("kt_scr", 16 * 1536)
    sp_v = Space("v_scr", 1536)

    es = ExitStack()
    with es:
        S = Sched(nc, es)

        def mark(name):
            PHASES.append((name, S.cnt['pe'], S.cnt['act'], S.cnt['dve']))

        def B(name, n, dt):
            return Buf(nc, es, name, n, dt)

        TW = TW0
        H = B("H", NCH * TW0, F32)
        XN = B("XN", NCH * TW0, BF16)
        ACTB = B("ACTB", NCH * TW0, BF16)
        WR = B("WR", NSLOT * SLOT, BF16)
        SG = B("SG", 2 * 512, F32)
        RS = B("RS", 2 * TW0, F32)
        SQB = B("SQB", 4 * 512, BF16)
        GV = B("GV", NGV, F32)
        ONES = B("ONES", 128, BF16)
        UB = B("UB", 770, F32)
        US = B("US", 18, F32)
        UE = B("UE", 18, F32)
        UPREV = B("UPREV", 32, F32)
        STT = B("STT", 32, F32)
        OCP = B("OCP", 32, F32)
        OCS = B("OCS", 32, F32)
        T1 = B("T1", 2 * 512, F32)
        TS = B("TS", 2 * 16, F32)
        STG = B("STG", 2 * 512, F32)
        STGB = B("STGB", 2 * 512, BF16)
        KST = B("KST", 16 * 16, BF16)
        VS = B("VS", D, BF16)
        VONES = B("VONES", 12 * 128, BF16)
        tail = NCH * TW1
        o = [tail]

        def carve(parent, n):
            v = View(parent, o[0], n)
            o[0] += n
            assert o[0] <= NCH * TW0
            return v
        EB = carve(H, 2 * 640)
        EBS = carve(H, 2 * 80)
        E = carve(H, 2 * 640)
        ESM = carve(H, 80)
        RDEN = carve(H, 512)
        RDENS = carve(H, 16)
        MASK = carve(H, 640)
        o[0] = tail
        KB = carve(XN, 2 * 1024)
        VB = carve(XN, 2 * 1024)
        o[0] = tail
        KC = carve(ACTB, 2 * 512)
        VC = carve(ACTB, 2 * 512)
        PM = carve(ACTB, 2 * 640)
        PSM = carve(ACTB, 80)

        PS = es.enter_context(nc.psum_tensor("PS", [128, 4096], F32))
        sp_ps = Space("psum", 8, exclusive=True)
        bank_ctr = [0]

        def bank():
            b = bank_ctr[0] % 8
            bank_ctr[0] += 1
            return b

        def pap(b, lo, hi, p0=0, p1=128):
            return PS[p0:p1, b * 512 + lo: b * 512 + hi]

        def prg(b):
            return (sp_ps, b, b + 1)

        def gcol(c):
            return GV.ap(c, c + 1)

        S.dma('sp', 'c_gv', GV.ap(0, NGV), gv_in, writes=[GV.rg(0, NGV)])
        S.dma('sp', 'c_stt', STT.ap(0, 32), stT, writes=[STT.rg(0, 32)])
        S.dma('pool', 'c_vones', VONES.ap(0, 1536), vones_in, writes=[VONES.rg(0, 1536)])
        S.op('dve', lambda e: e.memset(ONES.ap(0, 128), 1.0), writes=[ONES.rg(0, 128)])

        def ffn_plan(ls):
            out = []
            for q in range(NQ):
                for (f0, n) in ((QCH * q, 4), (QCH * q + 4, 4), (QCH * q + 8, 3)):
                    out.append(('g', ls, f0, n))
                    out.append(('u', ls, f0, n))
                for dg in range(4):
                    out.append(('d', ls, q, dg))
            return out

        plan = []
        for t in range(2):
            plan += ffn_plan(0)
            for hs in range(8):
                plan += [('incx', hs), ('inb', hs)]
            for g in range(4):
                plan.append(('out', g))
            plan += ffn_plan(1)
            for g in range(4):
                plan.append(('k', g))
            for g in range(4):
                plan.append(('v', g))
        for t in range(2):
            plan += ffn_plan(2)
            for g in range(4):
                plan.append(('q', g))
            for g in range(4):
                plan.append(('o', g))
            plan += ffn_plan(3)

        def wsrc(item):
            kind = item[0]
            if kind == 'incx':
                hs = item[1]
                out = []
                for pi, part in enumerate((1, 2)):
                    c0 = part * D + hs * 256
                    out.append((w_in[:, c0:c0 + 256].rearrange("(k p) w -> p k w", p=128), 16, 256, pi * 4096))
                return out
            if kind == 'inb':
                hs = item[1]
                c0 = hs * 256
                return [(w_in[:, c0:c0 + 256].rearrange("(k p) w -> p k w", p=128), 16, 256, 0)]
            src, k, w = wsrc1(item)
            return [(src, k, w, 0)]

        def wsrc1(item):
            kind = item[0]
            if kind in ('g', 'u'):
                _, ls, f0, n = item
                w = (w_gate if kind == 'g' else w_up)[ls]
                return w[:, f0 * 128:(f0 + n) * 128].rearrange("(k p) w -> p k w", p=128), 16, n * 128
            if kind == 'd':
                _, ls, q, dg = item
                w = w_down[ls]
                return (w[QCH * q * 128:(QCH * q + QCH) * 128, dg * 512:(dg + 1) * 512]
                        .rearrange("(k p) w -> p k w", p=128), QCH, 512)
            if kind == 'in':
                _, part, g = item
                c0 = part * D + g * 512
                return w_in[:, c0:c0 + 512].rearrange("(k p) w -> p k w", p=128), 16, 512
            w = {'out': w_out, 'q': w_q, 'o': w_o}.get(kind)
            if w is not None:
                g = item[1]
                return w[:, g * 512:(g + 1) * 512].rearrange("(k p) w -> p k w", p=128), 16, 512
            g = item[1]
            c0 = g * 512 + (D if kind == 'v' else 0)
            return w_kv[:, c0:c0 + 512].rearrange("(k p) w -> p k w", p=128), 16, 512

        ws = {'issued': 0, 'next': 0}

        def ws_issue_upto(n):
            while ws['issued'] < min(n, len(plan)):
                i = ws['issued']
                s = i % NSLOT
                base = s * SLOT
                pieces = wsrc(plan[i])
                for (src, k, w, off) in pieces:
                    dst = WR.t[:, base + off:base + off + k * w].rearrange("p (k w) -> p k w", w=w)
                    S.dma('pool', 'c_wr%d' % s, dst, src, writes=[WR.rg(base + off, base + off + k * w)])
                if len(pieces) > 1:
                    last = ('c_wr%d' % s, 16 * S.chan['c_wr%d' % s][1])
                    for (src, k, w, off) in pieces:
                        WR.sp.access(base + off, base + off + k * w, True, last)
                ws['issued'] += 1

        def ws_next(kind):
            i = ws['next']
            assert plan[i][0] == kind, (plan[i], kind)
            ws_issue_upto(i + 1)
            ws['next'] += 1
            s = i % NSLOT
            _, k, w, _ = wsrc(plan[i])[0]
            return s * SLOT, k, w

        def ws_release():
            ws_issue_upto(ws['next'] + NSLOT - 1)

        ws_release()

        def cgroups(ne):
            cg = [(0, 512)]
            if ne:
                cg.append((512, ne))
            return cg

        def mm_group(cgs, nk, lhs_fn, rhs_fn, lhs_rg, rhs_rg_fn, banks=None):
            if banks is None:
                banks = [bank() for _ in cgs]
            for kk in range(nk):
                for (c0, n), b in zip(cgs, banks):
                    S.op('pe',
                         lambda e, kk=kk, c0=c0, n=n, b=b: e.matmul(
                             pap(b, 0, n), lhsT=lhs_fn(kk), rhs=rhs_fn(kk, c0, n),
                             start=(kk == 0), stop=(kk == nk - 1)),
                         reads=[lhs_rg(kk), rhs_rg_fn(kk, c0, n)], writes=[prg(b)])
            return banks

        def rmsnorm(gc, W, cgs, squares_done=False):
            for i in range(0 if not squares_done else NCH, NCH):
                if i % 2 == 0:
                    S.op('act', lambda e, i=i: e.activation(
                        out=XN.ap(i * TW, i * TW + W), in_=H.ap(i * TW, i * TW + W), func=AF.Square),
                        reads=[H.rg(i * TW, i * TW + W)], writes=[XN.rg(i * TW, i * TW + W)])
                else:
                    S.op('dve', lambda e, i=i: e.tensor_tensor(
                        out=XN.ap(i * TW, i * TW + W), in0=H.ap(i * TW, i * TW + W),
                        in1=H.ap(i * TW, i * TW + W), op=ALU.mult),
                        reads=[H.rg(i * TW, i * TW + W)], writes=[XN.rg(i * TW, i * TW + W)])
            banks = mm_group(cgs, NCH,
                             lambda kk: ONES.ap(0, 128),
                             lambda kk, c0, n: XN.ap(kk * TW + c0, kk * TW + c0 + n),
                             lambda kk: ONES.rg(0, 128),
                             lambda kk, c0, n: XN.rg(kk * TW + c0, kk * TW + c0 + n))
            for (c0, n), b in zip(cgs, banks):
                S.op('act', lambda e, c0=c0, n=n, b=b: e.activation(
                    out=RS.ap(c0, c0 + n), in_=pap(b, 0, n), func=AF.Ln,
                    scale=1.0 / D, bias=EPS),
                    reads=[prg(b)], writes=[RS.rg(c0, c0 + n)])
            S.op('act', lambda e: e.activation(out=RS.ap(TW, TW + W), in_=RS.ap(0, W), func=AF.Exp, scale=-0.5),
                 reads=[RS.rg(0, W)], writes=[RS.rg(TW, TW + W)])
            for i in range(NCH):
                S.op('dve', lambda e, i=i: e.scalar_tensor_tensor(
                    out=XN.ap(i * TW, i * TW + W), in0=H.ap(i * TW, i * TW + W),
                    scalar=gcol(gc + i), in1=RS.ap(TW, TW + W), op0=ALU.mult, op1=ALU.mult),
                    reads=[H.rg(i * TW, i * TW + W), RS.rg(TW, TW + W), GV.rg(gc + i, gc + i + 1)],
                    writes=[XN.rg(i * TW, i * TW + W)])

        sg_ctr = [0]

        def early_square(W):
            def f(i):
                S.op('act', lambda e: e.activation(
                    out=XN.ap(i * TW, i * TW + W), in_=H.ap(i * TW, i * TW + W), func=AF.Square),
                    reads=[H.rg(i * TW, i * TW + W)], writes=[XN.rg(i * TW, i * TW + W)])
            return f

        def ffn(ls, W, cgs, on_final=None, squares_done=False):
            rmsnorm(GC_FFN + ls * 16, W, cgs, squares_done)
            for q in range(NQ):
                for (f0, n) in ((QCH * q, 4), (QCH * q + 4, 4), (QCH * q + 8, 3)):
                    gb, gk, gw = ws_next('g')
                    ub, uk, uw = ws_next('u')
                    for jj in range(n):
                        a = f0 + jj - QCH * q
                        bg = mm_group(cgs, NCH,
                                      lambda kk: WR.ap(gb + kk * gw + jj * 128, gb + kk * gw + jj * 128 + 128),
                                      lambda kk, c0, nn: XN.ap(kk * TW + c0, kk * TW + c0 + nn),
                                      lambda kk: WR.rg(gb, gb + SLOT),
                                      lambda kk, c0, nn: XN.rg(kk * TW + c0, kk * TW + c0 + nn))
                        bu = mm_group(cgs, NCH,
                                      lambda kk: WR.ap(ub + kk * uw + jj * 128, ub + kk * uw + jj * 128 + 128),
                                      lambda kk, c0, nn: XN.ap(kk * TW + c0, kk * TW + c0 + nn),
                                      lambda kk: WR.rg(ub, ub + SLOT),
                                      lambda kk, c0, nn: XN.rg(kk * TW + c0, kk * TW + c0 + nn))
                        for (c0, nn), b1, b2 in zip(cgs, bg, bu):
                            so = (sg_ctr[0] % 2) * 512
                            sg_ctr[0] += 1
                            S.op('act', lambda e, so=so, nn=nn, b1=b1: e.activation(
                                out=SG.ap(so, so + nn), in_=pap(b1, 0, nn), func=AF.Silu),
                                reads=[prg(b1)], writes=[SG.rg(so, so + nn)])
                            S.op('dve', lambda e, so=so, c0=c0, nn=nn, b2=b2: e.tensor_tensor(
                                out=ACTB.ap(a * TW + c0, a * TW + c0 + nn),
                                in0=SG.ap(so, so + nn), in1=pap(b2, 0, nn), op=ALU.mult),
                                reads=[SG.rg(so, so + nn), prg(b2)],
                                writes=[ACTB.rg(a * TW + c0, a * TW + c0 + nn)])
                    ws_release()
                for dg in range(4):
                    db, dk, dw = ws_next('d')
                    for jj in range(4):
                        i = dg * 4 + jj
                        bs = mm_group(cgs, QCH,
                                      lambda kk: WR.ap(db + kk * dw + jj * 128, db + kk * dw + jj * 128 + 128),
                                      lambda kk, c0, nn: ACTB.ap(kk * TW + c0, kk * TW + c0 + nn),
                                      lambda kk: WR.rg(db, db + SLOT),
                                      lambda kk, c0, nn: ACTB.rg(kk * TW + c0, kk * TW + c0 + nn))
                        for (c0, nn), b in zip(cgs, bs):
                            S.op('dve', lambda e, c0=c0, nn=nn, b=b: e.scalar_tensor_tensor(
                                out=H.ap(i * TW + c0, i * TW + c0 + nn), in0=pap(b, 0, nn), scalar=0.5,
                                in1=H.ap(i * TW + c0, i * TW + c0 + nn), op0=ALU.mult, op1=ALU.add),
                                reads=[prg(b), H.rg(i * TW + c0, i * TW + c0 + nn)],
                                writes=[H.rg(i * TW + c0, i * TW + c0 + nn)])
                        if q == NQ - 1 and on_final is not None:
                            on_final(i)
                    ws_release()

        def proj_residual(kind, src, W, cgs):
            for g in range(4):
                wb, wk, ww = ws_next(kind)
                for jj in range(4):
                    i = g * 4 + jj
                    bs = mm_group(cgs, NCH,
                                  lambda kk: WR.ap(wb + kk * ww + jj * 128, wb + kk * ww + jj * 128 + 128),
                                  lambda kk, c0, nn: src.ap(kk * TW + c0, kk * TW + c0 + nn),
                                  lambda kk: WR.rg(wb, wb + SLOT),
                                  lambda kk, c0, nn: src.rg(kk * TW + c0, kk * TW + c0 + nn))
                    for (c0, nn), b in zip(cgs, bs):
                        S.op('dve', lambda e, c0=c0, nn=nn, b=b: e.tensor_tensor(
                            out=H.ap(i * TW + c0, i * TW + c0 + nn), in0=pap(b, 0, nn),
                            in1=H.ap(i * TW + c0, i * TW + c0 + nn), op=ALU.add),
                            reads=[prg(b), H.rg(i * TW + c0, i * TW + c0 + nn)],
                            writes=[H.rg(i * TW + c0, i * TW + c0 + nn)])
                ws_release()

        sq_ctr = [0]

        def headnorm_a(b, n):
            sq = (sq_ctr[0] % 4) * 512
            sq_ctr[0] += 1
            S.op('act', lambda e: e.activation(out=SQB.ap(sq, sq + n), in_=pap(b, 0, n), func=AF.Square),
                 reads=[prg(b)], writes=[SQB.rg(sq, sq + n)])
            return sq

        def headnorm_b(b, sq, c0, n, gcolumn, out_fn, out_rg, sb=None):
            if sb is None:
                sb = bank()
            S.op('pe', lambda e: e.matmul(pap(sb, 0, n), lhsT=ONES.ap(0, 128), rhs=SQB.ap(sq, sq + n),
                                          start=True, stop=True),
                 reads=[ONES.rg(0, 128), SQB.rg(sq, sq + n)], writes=[prg(sb)])
            S.op('act', lambda e: e.activation(out=RS.ap(c0, c0 + n), in_=pap(sb, 0, n), func=AF.Ln,
                                               scale=1.0 / 128, bias=EPS),
                 reads=[prg(sb)], writes=[RS.rg(c0, c0 + n)])
            S.op('act', lambda e: e.activation(out=RS.ap(TW + c0, TW + c0 + n), in_=RS.ap(c0, c0 + n),
                                               func=AF.Exp, scale=-0.5),
                 reads=[RS.rg(c0, c0 + n)], writes=[RS.rg(TW + c0, TW + c0 + n)])
            S.op('dve', lambda e: e.scalar_tensor_tensor(
                out=out_fn(), in0=pap(b, 0, n), scalar=gcol(gcolumn),
                in1=RS.ap(TW + c0, TW + c0 + n), op0=ALU.mult, op1=ALU.mult),
                reads=[prg(b), RS.rg(TW + c0, TW + c0 + n), GV.rg(gcolumn, gcolumn + 1)],
                writes=[out_rg])

        xT3 = xT.rearrange("(k p) t -> p k t", p=128)
        H3 = H.t[:, 0:NCH * TW0].rearrange("p (k w) -> p k w", w=TW0)
        hs3 = h_scr.rearrange("p (k w) -> p k w", w=1040)
        yT3 = yT.rearrange("(k p) t -> p k t", p=128)
        stg_ctr = [0]
        NM = 768

        def all_h(c0, c1):
            return [H.rg(i * TW + c0, i * TW + c1) for i in range(NCH)]

        def sgslot():
            so = (sg_ctr[0] % 2) * 512
            sg_ctr[0] += 1
            return so

        for t in range(2):
            ne = 18 if t == 0 else 0
            W = NM + ne
            cgs = [(0, 512), (512, W - 512)]
            tk0 = NM * t
            if t == 0:
                S.dma('sp', 'c_h', H3[:, :, 0:NM], xT3[:, :, tk0:tk0 + NM], writes=all_h(0, NM))
                S.dma('sp', 'c_hx', H3[:, :, NM:NM + 18], xT3[:, :, 1536:1554], writes=all_h(NM, NM + 18))
            mark('L0t%d ffn0' % t)
            ffn(0, W, cgs, on_final=early_square(W))
            mark('L0t%d conv' % t)
            rmsnorm(GC_MIX + 0, W, cgs, squares_done=True)
            for g in range(8):
                cb, _, cw = ws_next('incx')
                xb, xw = cb + 4096, cw
                bb, _, bw = ws_next('inb')
                for jj in range(2):
                    i = g * 2 + jj

                    def wl(base, w):
                        return (lambda kk: WR.ap(base + kk * w + jj * 128, base + kk * w + jj * 128 + 128),
                                lambda kk: WR.rg(base, base + SLOT))
                    rf = lambda kk, c0, nn: XN.ap(kk * TW + c0, kk * TW + c0 + nn)
                    rr = lambda kk, c0, nn: XN.rg(kk * TW + c0, kk * TW + c0 + nn)
                    l1, l2 = wl(cb, cw)
                    bc = mm_group(cgs, NCH, l1, rf, l2, rr)
                    l1, l2 = wl(xb, xw)
                    bx = mm_group(cgs, NCH, l1, rf, l2, rr)
                    l1, l2 = wl(bb, bw)
                    bbk = mm_group(cgs, NCH, l1, rf, l2, rr)
                    for (c0, nn), bC, bX in zip(cgs, bc, bx):
                        so = sgslot()
                        S.op('act', lambda e, so=so, nn=nn, bC=bC: e.activation(
                            out=SG.ap(so, so + nn), in_=pap(bC, 0, nn), func=AF.Copy),
                            reads=[prg(bC)], writes=[SG.rg(so, so + nn)])
                        nmain = min(c0 + nn, NM) - c0
                        S.op('dve', lambda e, so=so, c0=c0, nmain=nmain, bX=bX: e.tensor_tensor(
                            out=UB.ap(2 + c0, 2 + c0 + nmain), in0=SG.ap(so, so + nmain),
                            in1=pap(bX, 0, nmain), op=ALU.mult),
                            reads=[SG.rg(so, so + nmain), prg(bX)], writes=[UB.rg(2 + c0, 2 + c0 + nmain)])
                        if nn > nmain:
                            S.op('dve', lambda e, so=so, nmain=nmain, bX=bX: e.tensor_tensor(
                                out=UE.ap(0, 18), in0=SG.ap(so + nmain, so + nmain + 18),
                                in1=pap(bX, nmain, nmain + 18), op=ALU.mult),
                                reads=[SG.rg(so + nmain, so + nmain + 18), prg(bX)], writes=[UE.rg(0, 18)])
                            S.op('dve', lambda e: e.tensor_copy(out=UB.ap(0, 2), in_=UE.ap(16, 18)),
                                 reads=[UE.rg(16, 18)], writes=[UB.rg(0, 2)])
                            S.op('dve', lambda e: e.tensor_copy(out=US.ap(2, 18), in_=UE.ap(0, 16)),
                                 reads=[UE.rg(0, 16)], writes=[US.rg(2, 18)])
                            S.op('dve', lambda e: e.tensor_copy(out=US.ap(0, 2), in_=STT.ap(2 * i, 2 * i + 2)),
                                 reads=[STT.rg(2 * i, 2 * i + 2)], writes=[US.rg(0, 2)])
                            S.op('dve', lambda e: e.tensor_copy(out=OCS.ap(2 * i, 2 * i + 2), in_=US.ap(16, 18)),
                                 reads=[US.rg(16, 18)], writes=[OCS.rg(2 * i, 2 * i + 2)])
                    if not ne:
                        S.op('dve', lambda e: e.tensor_copy(out=UB.ap(0, 2), in_=UPREV.ap(2 * i, 2 * i + 2)),
                             reads=[UPREV.rg(2 * i, 2 * i + 2)], writes=[UB.rg(0, 2)])
                    S.op('dve', lambda e: e.tensor_copy(out=UPREV.ap(2 * i, 2 * i + 2), in_=UB.ap(NM, NM + 2)),
                         reads=[UB.rg(NM, NM + 2)], writes=[UPREV.rg(2 * i, 2 * i + 2)])
                    if t == 1:
                        S.op('dve', lambda e: e.tensor_copy(out=OCP.ap(2 * i, 2 * i + 2), in_=UB.ap(NM, NM + 2)),
                             reads=[UB.rg(NM, NM + 2)], writes=[OCP.rg(2 * i, 2 * i + 2)])

                    def conv(src, s0, n, tmp, bbank, bc0, out_lo):
                        w0 = gcol(GC_CONV + 0 * 16 + i)
                        w1 = gcol(GC_CONV + 1 * 16 + i)
                        w2 = gcol(GC_CONV + 2 * 16 + i)
                        gr = [GV.rg(GC_CONV, GC_CONV + 48)]
                        A = (0, n)
                        Bq = (tmp.n // 2, tmp.n // 2 + n)
                        S.op('dve', lambda e: e.tensor_scalar(
                            out=tmp.ap(*A), in0=src.ap(s0, s0 + n), scalar1=w0, scalar2=None, op0=ALU.mult),
                            reads=[src.rg(s0, s0 + n)] + gr, writes=[tmp.rg(*A)])
                        S.op('dve', lambda e: e.scalar_tensor_tensor(
                            out=tmp.ap(*Bq), in0=src.ap(s0 + 1, s0 + 1 + n), scalar=w1, in1=tmp.ap(*A),
                            op0=ALU.mult, op1=ALU.add),
                            reads=[src.rg(s0 + 1, s0 + 1 + n), tmp.rg(*A)] + gr, writes=[tmp.rg(*Bq)])
                        S.op('dve', lambda e: e.scalar_tensor_tensor(
                            out=tmp.ap(*A), in0=src.ap(s0 + 2, s0 + 2 + n), scalar=w2, in1=tmp.ap(*Bq),
                            op0=ALU.mult, op1=ALU.add),
                            reads=[src.rg(s0 + 2, s0 + 2 + n), tmp.rg(*Bq)] + gr, writes=[tmp.rg(*A)])
                        S.op('dve', lambda e: e.tensor_tensor(
                            out=ACTB.ap(out_lo, out_lo + n), in0=tmp.ap(*A), in1=pap(bbank, bc0, bc0 + n), op=ALU.mult),
                            reads=[tmp.rg(*A), prg(bbank)], writes=[ACTB.rg(out_lo, out_lo + n)])

                    conv(UB, 0, 512, T1, bbk[0], 0, i * TW)
                    conv(UB, 512, 256, T1, bbk[1], 0, i * TW + 512)
                    if ne:
                        conv(US, 0, 16, TS, bbk[1], 256, i * TW + NM)
                        S.op('dve', lambda e: e.tensor_copy(out=ACTB.ap(i * TW + NM + 16, i * TW + NM + 18),
                                                            in_=pap(bbk[1], 272, 274)),
                             reads=[prg(bbk[1])], writes=[ACTB.rg(i * TW + NM + 16, i * TW + NM + 18)])
                ws_release()
            mark('L0t%d wout' % t)
            proj_residual('out', ACTB, W, cgs)
            mark('L0t%d ffn1' % t)
            ffn(1, W, cgs, on_final=early_square(W))
            if t == 0:
                S.dma('sp', 'c_hs', hs3[:, :, 0:256], H3[:, :, 512:768], reads=all_h(512, 768),
                      writes=[(sp_hscr, i * 1040, i * 1040 + 256) for i in range(NCH)])
                S.dma('sp', 'c_hs2', hs3[:, :, 1024:1040], H3[:, :, NM:NM + 16], reads=all_h(NM, NM + 16),
                      writes=[(sp_hscr, i * 1040 + 1024, i * 1040 + 1040) for i in range(NCH)])
            else:
                S.dma('sp', 'c_hs', hs3[:, :, 256:1024], H3[:, :, 0:NM], reads=all_h(0, NM),
                      writes=[(sp_hscr, i * 1040 + 256, i * 1040 + 1024) for i in range(NCH)])
            mark('L0t%d kv' % t)
            rmsnorm(GC_KV, W, cgs, squares_done=True)
            if t == 0:
                S.dma('sp', 'c_h', H3[:, :, 0:NM], xT3[:, :, NM:2 * NM], writes=all_h(0, NM))
            else:
                def l1h(c0, c1):
                    return [H.rg(i * TW1 + c0, i * TW1 + c1) for i in range(NCH)]
                H3b = H.t[:, 0:NCH * TW1].rearrange("p (k w) -> p k w", w=TW1)
                S.dma('sp', 'c_h', H3b[:, :, 0:512], hs3[:, :, 0:512],
                      reads=[(sp_hscr, i * 1040, i * 1040 + 512) for i in range(NCH)], writes=l1h(0, 512))
                S.dma('sp', 'c_hx', H3b[:, :, 512:528], hs3[:, :, 1024:1040],
                      reads=[(sp_hscr, i * 1040 + 1024, i * 1040 + 1040) for i in range(NCH)], writes=l1h(512, 528))
            pending = [None]

            def k_post(hd, bs, sqs, t=t, cgs=cgs, tk0=tk0):
                for (c0, nn), b, sq in zip(cgs, bs, sqs):
                    st = (stg_ctr[0] % 2) * 512
                    sl = stg_ctr[0] % 2
                    stg_ctr[0] += 1
                    headnorm_b(b, sq, c0, nn, GC_KG, lambda: STG.ap(st, st + nn), STG.rg(st, st + nn),
                               sb=6 + (0 if c0 == 0 else 1))
                    S.op('act', lambda e: e.activation(out=STGB.ap(st, st + nn), in_=STG.ap(st, st + nn), func=AF.Copy),
                         reads=[STG.rg(st, st + nn)], writes=[STGB.rg(st, st + nn)])
                    nmain = min(c0 + nn, NM) - c0
                    r0 = tk0 + c0
                    S.dma('sp', 'c_stgb%d' % sl, kt_scr[hd, :, r0:r0 + nmain], STGB.ap(st, st + nmain),
                          reads=[STGB.rg(st, st + nmain)], writes=[(sp_kt, hd * 1536 + r0, hd * 1536 + r0 + nmain)])
                    if t == 1:
                        lo, hi = max(c0, 256), c0 + nmain
                        if hi > lo:
                            S.dma('sp', 'c_stg%d' % sl, kpT[hd * 128:(hd + 1) * 128, lo - 256:hi - 256],
                                  STG.ap(st + lo - c0, st + hi - c0), reads=[STG.rg(st + lo - c0, st + hi - c0)])
                    if nn > nmain:
                        S.op('act', lambda e: e.activation(out=KST.ap(hd * 16, hd * 16 + 16),
                                                           in_=STG.ap(st + nmain, st + nmain + 16), func=AF.Copy),
                             reads=[STG.rg(st + nmain, st + nmain + 16)], writes=[KST.rg(hd * 16, hd * 16 + 16)])
                        S.dma('sp', 'c_stg%d' % sl, ksT[hd * 128:(hd + 1) * 128, :], STG.ap(st + nmain, st + nmain + 16),
                              reads=[STG.rg(st + nmain, st + nmain + 16)])

            for g in range(4):
                kb_, _, kw = ws_next('k')
                for jj in range(4):
                    hd = g * 4 + jj
                    bs = mm_group(cgs, NCH,
                                  lambda kk: WR.ap(kb_ + kk * kw + jj * 128, kb_ + kk * kw + jj * 128 + 128),
                                  lambda kk, c0, nn: XN.ap(kk * TW + c0, kk * TW + c0 + nn),
                                  lambda kk: WR.rg(kb_, kb_ + SLOT),
                                  lambda kk, c0, nn: XN.rg(kk * TW + c0, kk * TW + c0 + nn),
                                  banks=[2 * (hd % 3), 2 * (hd % 3) + 1][:len(cgs)])
                    sqs = [headnorm_a(b, nn) for (c0, nn), b in zip(cgs, bs)]
                    if pending[0] is not None:
                        k_post(*pending[0])
                    pending[0] = (hd, bs, sqs)
                ws_release()
            k_post(*pending[0])
            for g in range(4):
                vb_, _, vw = ws_next('v')
                blocks = [(tb * 128, 128) for tb in range(NM // 128)] + ([(NM, 16)] if ne else [])
                for (c0, nt) in blocks:
                    b = bank()
                    for kk in range(NCH):
                        S.op('pe', lambda e, kk=kk: e.matmul(
                            pap(b, 0, 512, 0, nt), lhsT=XN.ap(kk * TW + c0, kk * TW + c0 + nt),
                            rhs=WR.ap(vb_ + kk * vw, vb_ + kk * vw + 512), start=(kk == 0), stop=(kk == NCH - 1)),
                            reads=[XN.rg(kk * TW + c0, kk * TW + c0 + nt), WR.rg(vb_, vb_ + SLOT)], writes=[prg(b)])
                    st = (stg_ctr[0] % 2) * 512
                    sl = stg_ctr[0] % 2
                    stg_ctr[0] += 1
                    if nt == 128:
                        S.op('act', lambda e: e.activation(out=STGB.ap(st, st + 512), in_=pap(b, 0, 512), func=AF.Copy),
                             reads=[prg(b)], writes=[STGB.rg(st, st + 512)])
                        r0 = tk0 + c0
                        S.dma('sp', 'c_stgb%d' % sl, v_scr[r0:r0 + 128, g * 512:(g + 1) * 512], STGB.ap(st, st + 512),
                              reads=[STGB.rg(st, st + 512)], writes=[(sp_v, r0, r0 + 128)])
                        if t == 1 and c0 >= 256:
                            S.op('dve', lambda e: e.tensor_copy(out=STG.ap(st, st + 512), in_=pap(b, 0, 512)),
                                 reads=[prg(b)], writes=[STG.rg(st, st + 512)])
                            S.dma('sp', 'c_stg%d' % sl, vp[c0 - 256:c0 - 256 + 128, g * 512:(g + 1) * 512],
                                  STG.ap(st, st + 512), reads=[STG.rg(st, st + 512)])
                    else:
                        S.op('act', lambda e: e.activation(out=VS.ap(g * 512, g * 512 + 512, 0, 16),
                                                           in_=pap(b, 0, 512, 0, 16), func=AF.Copy),
                             reads=[prg(b)], writes=[VS.rg(g * 512, g * 512 + 512)])
                        S.op('dve', lambda e: e.tensor_copy(out=STG.ap(st, st + 512, 0, 16), in_=pap(b, 0, 512, 0, 16)),
                             reads=[prg(b)], writes=[STG.rg(st, st + 512)])
                        S.dma('sp', 'c_stg%d' % sl, vs[:, g * 512:(g + 1) * 512], STG.ap(st, st + 512, 0, 16),
                              reads=[STG.rg(st, st + 512)])
                ws_release()
            if t == 1:
                S.dma('sp', 'c_oc', convp, OCP.ap(0, 32), reads=[OCP.rg(0, 32)])
                S.dma('sp', 'c_oc', convs, OCS.ap(0, 32), reads=[OCS.rg(0, 32)])

        TW = TW1
        H3 = H.t[:, 0:NCH * TW1].rearrange("p (k w) -> p k w", w=TW1)
        QT = ACTB
        ATT = XN
        S.dma('sp', 'c_mask', MASK.ap(0, 640), maskT, writes=[MASK.rg(0, 640)])
        for t in range(2):
            ne = 16 if t == 0 else 0
            W = 512 + ne
            cgs = [(0, 384), (384, 144)] if ne else [(0, 512)]
            O = 512 + 512 * t
            if t == 1:
                S.dma('sp', 'c_h', H3[:, :, 0:512], hs3[:, :, 512:1024],
                      reads=[(sp_hscr, i * 1040 + 512, i * 1040 + 1024) for i in range(NCH)],
                      writes=all_h(0, 512))
            mark('L1t%d ffn0' % t)
            ffn(2, W, cgs, on_final=early_square(W))
            mark('L1t%d q' % t)
            rmsnorm(GC_MIX + 16, W, cgs, squares_done=True)
            pending = [None]

            def q_post(hd, bs, sqs, cgs=cgs):
                for (c0, nn), b, sq in zip(cgs, bs, sqs):
                    headnorm_b(b, sq, c0, nn, GC_QG,
                               lambda c0=c0, nn=nn: QT.ap(hd * TW + c0, hd * TW + c0 + nn),
                               QT.rg(hd * TW + c0, hd * TW + c0 + nn), sb=6 + (0 if c0 == 0 else 1))

            for g in range(4):
                qb_, _, qw = ws_next('q')
                for jj in range(4):
                    hd = g * 4 + jj
                    bs = mm_group(cgs, NCH,
                                  lambda kk: WR.ap(qb_ + kk * qw + jj * 128, qb_ + kk * qw + jj * 128 + 128),
                                  lambda kk, c0, nn: XN.ap(kk * TW + c0, kk * TW + c0 + nn),
                                  lambda kk: WR.rg(qb_, qb_ + SLOT),
                                  lambda kk, c0, nn: XN.rg(kk * TW + c0, kk * TW + c0 + nn),
                                  banks=[2 * (hd % 3), 2 * (hd % 3) + 1][:len(cgs)])
                    sqs = [headnorm_a(b, nn) for (c0, nn), b in zip(cgs, bs)]
                    if pending[0] is not None:
                        q_post(*pending[0])
                    pending[0] = (hd, bs, sqs)
                ws_release()
            q_post(*pending[0])
            mark('L1t%d attn' % t)
            def head_setup(hd, O=O, ne=ne):
                hb = hd % 2
                k0, v0, e0 = hb * 1024, hb * 1024, hb * 640
                S.dma('sp', 'c_kb%d' % hb, KB.ap(k0, k0 + 1024), kt_scr[hd, :, O - 512:O + 512],
                      reads=[(sp_kt, hd * 1536 + O - 512, hd * 1536 + O + 512)], writes=[KB.rg(k0, k0 + 1024)])
                S.dma('sp', 'c_vb%d' % hb,
                      VB.sl(v0, v0 + 1024).rearrange("p (b d) -> p b d", d=128),
                      v_scr[O - 512:O + 512, hd * 128:(hd + 1) * 128].rearrange("(b p) d -> p b d", p=128),
                      reads=[(sp_v, O - 512, O + 512)], writes=[VB.rg(v0, v0 + 1024)])
                S.dma('sp', 'c_eb%d' % hb, EB.ap(e0, e0 + 640), biasT[hd], writes=[EB.rg(e0, e0 + 640)])
                S.op('act', lambda e: e.activation(out=EB.ap(e0, e0 + 640), in_=EB.ap(e0, e0 + 640), func=AF.Exp),
                     reads=[EB.rg(e0, e0 + 640)], writes=[EB.rg(e0, e0 + 640)])
                S.op('dve', lambda e: e.tensor_tensor(out=EB.ap(e0, e0 + 640), in0=EB.ap(e0, e0 + 640),
                                                      in1=MASK.ap(0, 640), op=ALU.mult),
                     reads=[EB.rg(e0, e0 + 640), MASK.rg(0, 640)], writes=[EB.rg(e0, e0 + 640)])
                if ne:
                    c0_, s0_ = hb * 512, hb * 80
                    S.dma('pool', 'c_kc%d' % hb, KC.ap(c0_, c0_ + 512), kcT[hd], writes=[KC.rg(c0_, c0_ + 512)])
                    S.dma('pool', 'c_vc%d' % hb,
                          VC.sl(c0_, c0_ + 512).rearrange("p (b d) -> p b d", d=128),
                          vc[:, hd * 128:(hd + 1) * 128].rearrange("(b p) d -> p b d", p=128),
                          writes=[VC.rg(c0_, c0_ + 512)])
                    S.dma('sp', 'c_ebs%d' % hb, EBS.ap(s0_, s0_ + 80), biasTs[hd], writes=[EBS.rg(s0_, s0_ + 80)])
                    S.op('act', lambda e: e.activation(out=EBS.ap(s0_, s0_ + 80), in_=EBS.ap(s0_, s0_ + 80), func=AF.Exp),
                         reads=[EBS.rg(s0_, s0_ + 80)], writes=[EBS.rg(s0_, s0_ + 80)])

            def unit_qk(hd, qb):
                if qb == 0:
                    head_setup(hd)
                hb = hd % 2
                k0 = hb * 1024
                sl3 = (hd * 4 + qb) % 3
                sA, sB = 2 * sl3, 2 * sl3 + 1
                qlo = hd * TW + qb * 128
                for jj in range(5):
                    kb = qb + jj
                    ob, oc = (sA, jj * 128) if jj < 4 else (sB, 0)
                    S.op('pe', lambda e, kb=kb, ob=ob, oc=oc: e.matmul(
                        pap(ob, oc, oc + 128), lhsT=KB.ap(k0 + kb * 128, k0 + kb * 128 + 128),
                        rhs=QT.ap(qlo, qlo + 128), start=True, stop=True),
                        reads=[KB.rg(k0 + kb * 128, k0 + kb * 128 + 128), QT.rg(qlo, qlo + 128)],
                        writes=[prg(ob)])

            def unit_soft(hd, qb):
                hb = hd % 2
                e0 = hb * 640
                par = qb % 2
                sl3 = (hd * 4 + qb) % 3
                sA, sB = 2 * sl3, 2 * sl3 + 1
                eo, po = par * 640, par * 640
                S.op('act', lambda e: e.activation(out=E.ap(eo, eo + 640), in_=pap(sA, 0, 640), func=AF.Exp, scale=SCALE),
                     reads=[prg(sA), prg(sB)], writes=[E.rg(eo, eo + 640)])
                S.op('dve', lambda e: e.tensor_tensor(out=PM.ap(po, po + 640), in0=E.ap(eo, eo + 640),
                                                      in1=EB.ap(e0, e0 + 640), op=ALU.mult),
                     reads=[E.rg(eo, eo + 640), EB.rg(e0, e0 + 640)], writes=[PM.rg(po, po + 640)])

            def head_release(hd):
                S.op('dve', lambda e: e.tensor_copy(out=T1.ap(0, 512), in_=pap(6, 0, 512)),
                     reads=[prg(6)], writes=[T1.rg(0, 512)])
                S.op('act', lambda e: e.activation(out=RDEN.ap(0, 512), in_=pap(7, 0, 512), func=AF.Ln),
                     reads=[prg(7)], writes=[RDEN.rg(0, 512)])

            def head_finish(hd):
                S.op('act', lambda e: e.activation(out=RDEN.ap(0, 512), in_=RDEN.ap(0, 512), func=AF.Exp, scale=-1.0),
                     reads=[RDEN.rg(0, 512)], writes=[RDEN.rg(0, 512)])
                S.op('dve', lambda e: e.tensor_tensor(out=ATT.ap(hd * TW, hd * TW + 512), in0=T1.ap(0, 512),
                                                      in1=RDEN.ap(0, 512), op=ALU.mult),
                     reads=[T1.rg(0, 512), RDEN.rg(0, 512)], writes=[ATT.rg(hd * TW, hd * TW + 512)])

            def unit_pv(hd, qb, O=O, ne=ne):
                hb = hd % 2
                v0 = hb * 1024
                par = qb % 2
                po = par * 640
                atb = 6
                denb = 7
                for jj in range(5):
                    kb = qb + jj
                    S.op('pe', lambda e, kb=kb, jj=jj: e.matmul(
                        pap(atb, qb * 128, qb * 128 + 128), lhsT=VB.ap(v0 + kb * 128, v0 + kb * 128 + 128),
                        rhs=PM.ap(po + jj * 128, po + jj * 128 + 128), start=(jj == 0), stop=(jj == 4)),
                        reads=[VB.rg(v0 + kb * 128, v0 + kb * 128 + 128), PM.rg(po + jj * 128, po + jj * 128 + 128)],
                        writes=[prg(atb)])
                for jj in range(5):
                    kbg = (O - 512) // 128 + qb + jj
                    S.op('pe', lambda e, kbg=kbg, jj=jj: e.matmul(
                        pap(denb, qb * 128, qb * 128 + 128), lhsT=VONES.ap(kbg * 128, kbg * 128 + 128),
                        rhs=PM.ap(po + jj * 128, po + jj * 128 + 128), start=(jj == 0), stop=(jj == 4)),
                        reads=[VONES.rg(kbg * 128, kbg * 128 + 128), PM.rg(po + jj * 128, po + jj * 128 + 128)],
                        writes=[prg(denb)])
                if qb != 3:
                    return
                if ne:
                    head_release(hd)
                    head_finish(hd)
                if ne:
                    c0_, s0_ = hb * 512, hb * 80
                    sS = 2 * ((hd * 4 + 3) % 3)
                    ats, dens = atb, denb
                    qlo = hd * TW + 512
                    for jj in range(4):
                        S.op('pe', lambda e, jj=jj: e.matmul(
                            pap(sS, jj * 16, jj * 16 + 16), lhsT=KC.ap(c0_ + jj * 128, c0_ + jj * 128 + 128),
                            rhs=QT.ap(qlo, qlo + 16), start=True, stop=True),
                            reads=[KC.rg(c0_ + jj * 128, c0_ + jj * 128 + 128), QT.rg(qlo, qlo + 16)],
                            writes=[prg(sS)])
                    S.op('pe', lambda e: e.matmul(
                        pap(sS, 64, 80, 0, 16), lhsT=KST.ap(hd * 16, hd * 16 + 16),
                        rhs=QT.ap(qlo, qlo + 16), start=True, stop=True),
                        reads=[KST.rg(hd * 16, hd * 16 + 16), QT.rg(qlo, qlo + 16)], writes=[prg(sS)])
                    S.op('act', lambda e: e.activation(out=ESM.ap(0, 64), in_=pap(sS, 0, 64), func=AF.Exp, scale=SCALE),
                         reads=[prg(sS)], writes=[ESM.rg(0, 64)])
                    S.op('act', lambda e: e.activation(out=ESM.ap(64, 80, 0, 16), in_=pap(sS, 64, 80, 0, 16), func=AF.Exp, scale=SCALE),
                         reads=[prg(sS)], writes=[ESM.rg(64, 80)])
                    S.op('dve', lambda e: e.tensor_tensor(out=PSM.ap(0, 64), in0=ESM.ap(0, 64), in1=EBS.ap(s0_, s0_ + 64), op=ALU.mult),
                         reads=[ESM.rg(0, 64), EBS.rg(s0_, s0_ + 64)], writes=[PSM.rg(0, 64)])
                    S.op('dve', lambda e: e.tensor_tensor(out=PSM.ap(64, 80, 0, 16), in0=ESM.ap(64, 80, 0, 16),
                                                          in1=EBS.ap(s0_ + 64, s0_ + 80, 0, 16), op=ALU.mult),
                         reads=[ESM.rg(64, 80), EBS.rg(s0_ + 64, s0_ + 80)], writes=[PSM.rg(64, 80)])
                    for (ob, lfn, lrg) in ((ats, lambda jj: VC.ap(c0_ + jj * 128, c0_ + jj * 128 + 128),
                                            lambda jj: VC.rg(c0_ + jj * 128, c0_ + jj * 128 + 128)),
                                           (dens, lambda jj: ONES.ap(0, 128), lambda jj: ONES.rg(0, 128))):
                        for jj in range(4):
                            S.op('pe', lambda e, jj=jj, ob=ob, lfn=lfn: e.matmul(
                                pap(ob, 0, 16), lhsT=lfn(jj), rhs=PSM.ap(jj * 16, jj * 16 + 16),
                                start=(jj == 0), stop=False),
                                reads=[lrg(jj), PSM.rg(jj * 16, jj * 16 + 16)], writes=[prg(ob)])
                        if ob == ats:
                            l5, l5r = VS.ap(hd * 128, hd * 128 + 128, 0, 16), VS.rg(hd * 128, hd * 128 + 128)
                        else:
                            l5, l5r = ONES.ap(0, 128, 0, 16), ONES.rg(0, 128)
                        S.op('pe', lambda e, ob=ob, l5=l5: e.matmul(
                            pap(ob, 0, 16), lhsT=l5, rhs=PSM.ap(64, 80, 0, 16), start=False, stop=True),
                            reads=[l5r, PSM.rg(64, 80)], writes=[prg(ob)])
                    S.op('act', lambda e: e.activation(out=RDENS.ap(0, 16), in_=pap(dens, 0, 16), func=AF.Ln),
                         reads=[prg(dens)], writes=[RDENS.rg(0, 16)])
                    S.op('act', lambda e: e.activation(out=RDENS.ap(0, 16), in_=RDENS.ap(0, 16), func=AF.Exp, scale=-1.0),
                         reads=[RDENS.rg(0, 16)], writes=[RDENS.rg(0, 16)])
                    S.op('dve', lambda e: e.tensor_tensor(out=ATT.ap(hd * TW + 512, hd * TW + 528), in0=pap(ats, 0, 16),
                                                          in1=RDENS.ap(0, 16), op=ALU.mult),
                         reads=[prg(ats), RDENS.rg(0, 16)], writes=[ATT.rg(hd * TW + 512, hd * TW + 528)])

            units = [(hd, qb) for hd in range(16) for qb in range(4)]
            unit_qk(*units[0])
            unit_qk(*units[1])
            prev = None
            for ui, u in enumerate(units):
                unit_soft(*u)
                if ui + 2 < len(units):
                    unit_qk(*units[ui + 2])
                if prev is not None:
                    head_release(prev)
                unit_pv(*u)
                if prev is not None:
                    head_finish(prev)
                    prev = None
                if u[1] == 3 and not ne:
                    prev = u[0]
            if prev is not None:
                head_release(prev)
                head_finish(prev)
            bank_ctr[0] = 0
            mark('L1t%d wo' % t)
            proj_residual('o', ATT, W, cgs)
            mark('L1t%d ffn1' % t)
            def y_store(i, t=t, ne=ne):
                S.dma('sp', 'c_y', yT[i * 128:(i + 1) * 128, 512 * t:512 * t + 512], H.ap(i * TW, i * TW + 512),
                      reads=[H.rg(i * TW, i * TW + 512)])
                if ne:
                    S.dma('sp', 'c_y2', yT[i * 128:(i + 1) * 128, 1024:1040], H.ap(i * TW + 512, i * TW + 528),
                          reads=[H.rg(i * TW + 512, i * TW + 528)])
            ffn(3, W, cgs, on_final=y_store)

        mark('end')
        assert ws['next'] == len(plan), (ws['next'], len(plan))
        S.final_wait('sp')
    return nc


def _vec16(v):
    return np.ascontiguousarray(np.asarray(v, np.float32).reshape(16, 128).T)


_NC_CACHE = {}


def kernel(x_prompt, x_sample, state_conv, cache_k, cache_v, ffn_norm, ffn_w_gate, ffn_w_up,
           ffn_w_down, mix_norm, conv_w_in, conv_w, conv_w_out, kv_norm, w_kv, k_gain,
           w_q, q_gain, rel_bias, w_o):
    f32 = np.float32
    x_prompt = np.asarray(x_prompt, f32)
    x_sample = np.asarray(x_sample, f32)
    state_conv = np.asarray(state_conv, f32)
    cache_k = np.asarray(cache_k, f32)
    cache_v = np.asarray(cache_v, f32)
    rel_bias = np.asarray(rel_bias, f32)

    gv = np.zeros((128, NGV), f32)
    fn = np.asarray(ffn_norm, f32)
    for l in range(2):
        for s in range(2):
            gv[:, GC_FFN + (l * 2 + s) * 16: GC_FFN + (l * 2 + s) * 16 + 16] = _vec16(fn[l, s])
        gv[:, GC_MIX + l * 16: GC_MIX + l * 16 + 16] = _vec16(np.asarray(mix_norm, f32)[l])
    gv[:, GC_KV:GC_KV + 16] = _vec16(kv_norm)
    cw = np.asarray(conv_w, f32)[0]
    for tap in range(3):
        gv[:, GC_CONV + tap * 16: GC_CONV + tap * 16 + 16] = _vec16(cw[tap])
    gv[:, GC_KG] = np.asarray(k_gain, f32)
    gv[:, GC_QG] = np.asarray(q_gain, f32)[0]

    rb = rel_bias[0]
    kk = np.arange(128)[:, None, None]
    jj = np.arange(5)[None, :, None]
    ql = np.arange(128)[None, None, :]
    dist = (4 - jj) * 128 + ql - kk
    idx = np.clip(dist, -128, 128) + 128
    biasT = np.ascontiguousarray(rb[:, idx].reshape(16, 128, 640))
    mask = np.ones((128, 5, 128), f32)
    inval0 = (kk < 64) & (ql >= 64)
    inval4 = (kk >= 64) & (ql < 64)
    mask[:, 0:1, :][np.broadcast_to(inval0, (128, 1, 128))] = 0.0
    mask[:, 4:5, :][np.broadcast_to(inval4, (128, 1, 128))] = 0.0
    maskT = np.ascontiguousarray(mask.reshape(128, 640))
    kidx = (np.arange(5)[None, :, None] * 128 + np.arange(128)[:, None, None])
    q = np.arange(16)[None, None, :]
    dist_s = q + 512 - kidx
    idx_s = np.clip(dist_s, -128, 128) + 128
    bts = rb[:, idx_s]
    valid_s = np.broadcast_to(kidx < 528, idx_s.shape)
    bts = np.where(valid_s[None], bts, 0.0).astype(f32)
    biasTs = np.ascontiguousarray(bts.reshape(16, 128, 80))

    wg = np.ascontiguousarray(np.asarray(ffn_w_gate, f32).reshape(4, D, DFF))
    wu = np.ascontiguousarray(np.asarray(ffn_w_up, f32).reshape(4, D, DFF))
    wd = np.ascontiguousarray(np.asarray(ffn_w_down, f32).reshape(4, DFF, D))
    shared = {
        "w_gate": wg, "w_up": wu, "w_down": wd,
        "w_in": np.ascontiguousarray(np.asarray(conv_w_in, f32)[0]),
        "w_out": np.ascontiguousarray(np.asarray(conv_w_out, f32)[0]),
        "w_kv": np.ascontiguousarray(np.asarray(w_kv, f32)),
        "w_q": np.ascontiguousarray(np.asarray(w_q, f32)[0]),
        "w_o": np.ascontiguousarray(np.asarray(w_o, f32)[0]),
        "gv": gv, "biasT": biasT, "biasTs": biasTs, "maskT": maskT,
    }

    in_maps = []
    for c in range(8):
        b, s = c // 4, c % 4
        start = 1024 * s
        xs = np.zeros((NTOK0, D), f32)
        if s > 0:
            xs[0:512] = x_prompt[b, start - 512:start]
            xs[1552:1554] = x_prompt[b, start - 514:start - 512]
        xs[512:1536] = x_prompt[b, start:start + 1024]
        xs[1536:1552] = x_sample[c]
        vones = np.ones((128, 12, 128), f32)
        if s == 0:
            vones[:, 0:4, :] = 0.0
        m = dict(shared)
        m["xT"] = np.ascontiguousarray(xs.T)
        m["stT"] = np.ascontiguousarray(
            state_conv[0, c].T.reshape(16, 128, 2).transpose(1, 0, 2).reshape(128, 32))
        m["kcT"] = np.ascontiguousarray(cache_k[c].transpose(1, 2, 0))
        m["vc"] = np.ascontiguousarray(cache_v[c].reshape(512, D))
        m["vones"] = np.ascontiguousarray(vones.reshape(128, 1536))
        in_maps.append(m)

    if "nc" not in _NC_CACHE:
        _NC_CACHE["nc"] = build_nc()
    nc = _NC_CACHE["nc"]
    res = run_bass_kernel_spmd(nc, in_maps, core_ids=list(range(8)))
    R = res.results

    y_prompt = np.zeros((2, 4096, D), f32)
    y_sample = np.zeros((8, 16, D), f32)
    conv_p = np.zeros((1, 2, 2, D), f32)
    k_p = np.zeros((2, 512, 16, 128), f32)
    v_p = np.zeros((2, 512, 16, 128), f32)
    conv_s = np.zeros((1, 8, 2, D), f32)
    k_s = np.zeros((8, 16, 16, 128), f32)
    v_s = np.zeros((8, 16, 16, 128), f32)

    def unvec(a):
        return a.reshape(128, 16, 2).transpose(2, 1, 0).reshape(2, D)

    for c in range(8):
        b, s = c // 4, c % 4
        r = R[c]
        yT = np.asarray(r["yT"])
        y_prompt[b, 1024 * s:1024 * s + 1024] = yT[:, :1024].T
        y_sample[c] = yT[:, 1024:1040].T
        conv_s[0, c] = unvec(np.asarray(r["convs"]))
        k_s[c] = np.asarray(r["ksT"]).T.reshape(16, 16, 128)
        v_s[c] = np.asarray(r["vs"]).reshape(16, 16, 128)
        if s == 3:
            conv_p[0, b] = unvec(np.asarray(r["convp"]))
            k_p[b] = np.asarray(r["kpT"]).T.reshape(512, 16, 128)
            v_p[b] = np.asarray(r["vp"]).reshape(512, 16, 128)
    return (y_prompt, y_sample, conv_p, k_p, v_p, conv_s, k_s, v_s)
```

```python
import numpy as np
from contextlib import ExitStack
import concourse.bass as bass
import concourse.mybir as mybir
from concourse.bass_utils import run_bass_kernel_spmd

F32 = mybir.dt.float32
BF16 = mybir.dt.bfloat16
AF = mybir.ActivationFunctionType
ALU = mybir.AluOpType

D = 2048
NCH = 16
DFF = 5632
NQ = 4
QCH = 11
TW0 = 786
TW1 = 528
NTOK0 = 1554
EPS = 1e-6
SCALE = 128.0 ** -0.5
SLOT = 8192
NSLOT = 4

GC_FFN = 0
GC_MIX = 64
GC_KV = 96
GC_CONV = 112
GC_KG = 160
GC_QG = 161
NGV = 162


PHASES = []


class Space:
    def __init__(self, name, size, exclusive=False):
        self.name = name
        self.exclusive = exclusive
        self.segs = [[0, size, None, {}]]

    def _split(self, pos):
        for i, s in enumerate(self.segs):
            if s[0] < pos < s[1]:
                self.segs.insert(i + 1, [pos, s[1], s[2], dict(s[3])])
                s[1] = pos
                return

    def access(self, lo, hi, write, tok):
        if self.exclusive:
            write = True
        self._split(lo)
        self._split(hi)
        deps = []
        first = None
        i = 0
        while i < len(self.segs):
            s = self.segs[i]
            if s[1] <= lo:
                i += 1
                continue
            if s[0] >= hi:
                break
            if s[2] is not None:
                deps.append((s[2][0], s[2][1], 'waw' if write else 'raw'))
            if write:
                for k, v in s[3].items():
                    deps.append((k, v, 'war'))
                if first is None:
                    first = i
                    s[2] = tok
                    s[3] = {}
                    i += 1
                else:
                    self.segs[first][1] = s[1]
                    del self.segs[i]
            else:
                k, v = tok
                if s[3].get(k, 0) < v:
                    s[3][k] = v
                i += 1
        return deps


class Sched:
    def __init__(self, nc, es):
        self.nc = nc
        self.es = es
        self.eng = {'pe': nc.tensor, 'act': nc.scalar, 'dve': nc.vector,
                    'pool': nc.gpsimd, 'sp': nc.sync}
        self.sem = {}
        for k in ('pe', 'act', 'dve'):
            self.sem[k] = es.enter_context(nc.semaphore('s_' + k))
        self.cnt = {k: 0 for k in self.eng}
        self.waited = {k: {} for k in self.eng}
        self.chan = {}
        self.nwait = 0

    def channel(self, name):
        if name not in self.chan:
            self.chan[name] = [self.es.enter_context(self.nc.semaphore('c_' + name)), 0]
        return name

    def _collect(self, eng, tok, reads, writes):
        deps = []
        for (sp, lo, hi) in reads:
            deps += sp.access(lo, hi, False, tok)
        for (sp, lo, hi) in writes:
            deps += sp.access(lo, hi, True, tok)
        best = {}
        for (k, v, kind) in deps:
            if k == eng:
                if eng == 'pe':
                    continue
                if v >= tok[1]:
                    continue
            if best.get(k, 0) < v:
                best[k] = v
        for k, v in best.items():
            if self.waited[eng].get(k, 0) >= v:
                continue
            self.waited[eng][k] = v
            sem = self.sem[k] if k in self.sem else self.chan[k][0]
            self.eng[eng].wait_ge(sem, v)
            self.nwait += 1

    def op(self, eng, fn, reads=(), writes=()):
        tok = (eng, self.cnt[eng] + 1)
        self._collect(eng, tok, reads, writes)
        inst = fn(self.eng[eng])
        inst.then_inc(self.sem[eng], 1)
        self.cnt[eng] += 1

    def dma(self, q, chname, out_ap, in_ap, reads=(), writes=()):
        ch = self.chan[self.channel(chname)]
        tok = (chname, 16 * (ch[1] + 1))
        self._collect(q, tok, reads, writes)
        self.eng[q].dma_start(out=out_ap, in_=in_ap).then_inc(ch[0], 16)
        ch[1] += 1

    def final_wait(self, eng='sp'):
        for name, (sem, cnt) in self.chan.items():
            if cnt > 0:
                self.eng[eng].wait_ge(sem, 16 * cnt)
        for k in ('pe', 'act', 'dve'):
            if self.cnt[k] > 0:
                self.eng[eng].wait_ge(self.sem[k], self.cnt[k])


class Buf:
    def __init__(self, nc, es, name, n, dt):
        self.t = es.enter_context(nc.sbuf_tensor(name, [128, n], dt))
        self.sp = Space(name, n)
        self.n = n

    def ap(self, lo, hi, p0=0, p1=128):
        return self.t[p0:p1, lo:hi]

    def rg(self, lo, hi):
        return (self.sp, lo, hi)

    def sl(self, lo, hi):
        return self.t[:, lo:hi]


class View:
    def __init__(self, parent, base, n):
        self.parent = parent
        self.base = base
        self.n = n
        self.sp = parent.sp

    def ap(self, lo, hi, p0=0, p1=128):
        return self.parent.ap(self.base + lo, self.base + hi, p0, p1)

    def rg(self, lo, hi):
        return self.parent.rg(self.base + lo, self.base + hi)

    def sl(self, lo, hi):
        return self.parent.t[:, self.base + lo:self.base + hi]


def build_nc():
    nc = bass.Bass("TRN2", target_bir_lowering=False)

    def din(name, shape, dt=F32):
        return nc.dram_tensor(name, list(shape), dt, kind="ExternalInput").ap()

    def dout(name, shape, dt=F32):
        return nc.dram_tensor(name, list(shape), dt, kind="ExternalOutput").ap()

    xT = din("xT", [D, NTOK0])
    stT = din("stT", [128, 32])
    kcT = din("kcT", [16, 128, 512])
    vc = din("vc", [512, D])
    w_gate = din("w_gate", [4, D, DFF])
    w_up = din("w_up", [4, D, DFF])
    w_down = din("w_down", [4, DFF, D])
    w_in = din("w_in", [D, 3 * D])
    w_out = din("w_out", [D, D])
    w_kv = din("w_kv", [D, 2 * D])
    w_q = din("w_q", [D, D])
    w_o = din("w_o", [D, D])
    gv_in = din("gv", [128, NGV])
    biasT = din("biasT", [16, 128, 640])
    biasTs = din("biasTs", [16, 128, 80])
    maskT = din("maskT", [128, 640])
    vones_in = din("vones", [128, 12 * 128])

    yT = dout("yT", [D, 1040])
    convp = dout("convp", [128, 32])
    convs = dout("convs", [128, 32])
    kpT = dout("kpT", [D, 512])
    vp = dout("vp", [512, D])
    ksT = dout("ksT", [D, 16])
    vs = dout("vs", [16, D])

    h_scr = nc.dram_tensor("h_scr", [128, NCH * 1040], F32, kind="Internal").ap()
    kt_scr = nc.dram_tensor("kt_scr", [16, 128, 1536], BF16, kind="Internal").ap()
    v_scr = nc.dram_tensor("v_scr", [1536, D], BF16, kind="Internal").ap()
    sp_hscr = Space("h_scr", NCH * 1040)
    sp_kt = Space("kt_scr", 16 * 1536)
    sp_v = Space("v_scr", 1536)

    es = ExitStack()
    with es:
        S = Sched(nc, es)

        def mark(name):
            PHASES.append((name, S.cnt['pe'], S.cnt['act'], S.cnt['dve']))

        def B(name, n, dt):
            return Buf(nc, es, name, n, dt)

        TW = TW0
        H = B("H", NCH * TW0, F32)
        XN = B("XN", NCH * TW0, BF16)
        ACTB = B("ACTB", NCH * TW0, BF16)
        WR = B("WR", NSLOT * SLOT, BF16)
        SG = B("SG", 2 * 512, F32)
        RS = B("RS", 2 * TW0, F32)
        SQB = B("SQB", 4 * 512, BF16)
        GV = B("GV", NGV, F32)
        ONES = B("ONES", 128, BF16)
        UB = B("UB", 770, F32)
        US = B("US", 18, F32)
        UE = B("UE", 18, F32)
        UPREV = B("UPREV", 32, F32)
        STT = B("STT", 32, F32)
        OCP = B("OCP", 32, F32)
        OCS = B("OCS", 32, F32)
        T1 = B("T1", 2 * 512, F32)
        TS = B("TS", 2 * 16, F32)
        STG = B("STG", 2 * 512, F32)
        STGB = B("STGB", 2 * 512, BF16)
        KST = B("KST", 16 * 16, BF16)
        VS = B("VS", D, BF16)
        VONES = B("VONES", 12 * 128, BF16)
        tail = NCH * TW1
        o = [tail]

        def carve(parent, n):
            v = View(parent, o[0], n)
            o[0] += n
            assert o[0] <= NCH * TW0
            return v
        EB = carve(H, 2 * 640)
        EBS = carve(H, 2 * 80)
        E = carve(H, 2 * 640)
        ESM = carve(H, 80)
        RDEN = carve(H, 512)
        RDENS = carve(H, 16)
        MASK = carve(H, 640)
        o[0] = tail
        KB = carve(XN, 2 * 1024)
        VB = carve(XN, 2 * 1024)
        o[0] = tail
        KC = carve(ACTB, 2 * 512)
        VC = carve(ACTB, 2 * 512)
        PM = carve(ACTB, 2 * 640)
        PSM = carve(ACTB, 80)

        PS = es.enter_context(nc.psum_tensor("PS", [128, 4096], F32))
        sp_ps = Space("psum", 8, exclusive=True)
        bank_ctr = [0]

        def bank():
            b = bank_ctr[0] % 8
            bank_ctr[0] += 1
            return b

        def pap(b, lo, hi, p0=0, p1=128):
            return PS[p0:p1, b * 512 + lo: b * 512 + hi]

        def prg(b):
            return (sp_ps, b, b + 1)

        def gcol(c):
            return GV.ap(c, c + 1)

        S.dma('sp', 'c_gv', GV.ap(0, NGV), gv_in, writes=[GV.rg(0, NGV)])
        S.dma('sp', 'c_stt', STT.ap(0, 32), stT, writes=[STT.rg(0, 32)])
        S.dma('pool', 'c_vones', VONES.ap(0, 1536), vones_in, writes=[VONES.rg(0, 1536)])
        S.op('dve', lambda e: e.memset(ONES.ap(0, 128), 1.0), writes=[ONES.rg(0, 128)])

        def ffn_plan(ls):
            out = []
            for q in range(NQ):
                for (f0, n) in ((QCH * q, 4), (QCH * q + 4, 4), (QCH * q + 8, 3)):
                    out.append(('g', ls, f0, n))
                    out.append(('u', ls, f0, n))
                for dg in range(4):
                    out.append(('d', ls, q, dg))
            return out

        plan = []
        for t in range(2):
            plan += ffn_plan(0)
            for hs in range(8):
                plan += [('incx', hs), ('inb', hs)]
            for g in range(4):
                plan.append(('out', g))
            plan += ffn_plan(1)
            for g in range(4):
                plan.append(('k', g))
            for g in range(4):
                plan.append(('v', g))
        for t in range(2):
            plan += ffn_plan(2)
            for g in range(4):
                plan.append(('q', g))
            for g in range(4):
                plan.append(('o', g))
            plan += ffn_plan(3)

        def wsrc(item):
            kind = item[0]
            if kind == 'incx':
                hs = item[1]
                out = []
                for pi, part in enumerate((1, 2)):
                    c0 = part * D + hs * 256
                    out.append((w_in[:, c0:c0 + 256].rearrange("(k p) w -> p k w", p=128), 16, 256, pi * 4096))
                return out
            if kind == 'inb':
                hs = item[1]
                c0 = hs * 256
                return [(w_in[:, c0:c0 + 256].rearrange("(k p) w -> p k w", p=128), 16, 256, 0)]
            src, k, w = wsrc1(item)
            return [(src, k, w, 0)]

        def wsrc1(item):
            kind = item[0]
            if kind in ('g', 'u'):
                _, ls, f0, n = item
                w = (w_gate if kind == 'g' else w_up)[ls]
                return w[:, f0 * 128:(f0 + n) * 128].rearrange("(k p) w -> p k w", p=128), 16, n * 128
            if kind == 'd':
                _, ls, q, dg = item
                w = w_down[ls]
                return (w[QCH * q * 128:(QCH * q + QCH) * 128, dg * 512:(dg + 1) * 512]
                        .rearrange("(k p) w -> p k w", p=128), QCH, 512)
            if kind == 'in':
                _, part, g = item
                c0 = part * D + g * 512
                return w_in[:, c0:c0 + 512].rearrange("(k p) w -> p k w", p=128), 16, 512
            w = {'out': w_out, 'q': w_q, 'o': w_o}.get(kind)
            if w is not None:
                g = item[1]
                return w[:, g * 512:(g + 1) * 512].rearrange("(k p) w -> p k w", p=128), 16, 512
            g = item[1]
            c0 = g * 512 + (D if kind == 'v' else 0)
            return w_kv[:, c0:c0 + 512].rearrange("(k p) w -> p k w", p=128), 16, 512

        ws = {'issued': 0, 'next': 0}

        def ws_issue_upto(n):
            while ws['issued'] < min(n, len(plan)):
                i = ws['issued']
                s = i % NSLOT
                base = s * SLOT
                pieces = wsrc(plan[i])
                for (src, k, w, off) in pieces:
                    dst = WR.t[:, base + off:base + off + k * w].rearrange("p (k w) -> p k w", w=w)
                    S.dma('pool', 'c_wr%d' % s, dst, src, writes=[WR.rg(base + off, base + off + k * w)])
                if len(pieces) > 1:
                    last = ('c_wr%d' % s, 16 * S.chan['c_wr%d' % s][1])
                    for (src, k, w, off) in pieces:
                        WR.sp.access(base + off, base + off + k * w, True, last)
                ws['issued'] += 1

        def ws_next(kind):
            i = ws['next']
            assert plan[i][0] == kind, (plan[i], kind)
            ws_issue_upto(i + 1)
            ws['next'] += 1
            s = i % NSLOT
            _, k, w, _ = wsrc(plan[i])[0]
            return s * SLOT, k, w

        def ws_release():
            ws_issue_upto(ws['next'] + NSLOT - 1)

        ws_release()

        def cgroups(ne):
            cg = [(0, 512)]
            if ne:
                cg.append((512, ne))
            return cg

        def mm_group(cgs, nk, lhs_fn, rhs_fn, lhs_rg, rhs_rg_fn, banks=None):
            if banks is None:
                banks = [bank() for _ in cgs]
            for kk in range(nk):
                for (c0, n), b in zip(cgs, banks):
                    S.op('pe',
                         lambda e, kk=kk, c0=c0, n=n, b=b: e.matmul(
                             pap(b, 0, n), lhsT=lhs_fn(kk), rhs=rhs_fn(kk, c0, n),
                             start=(kk == 0), stop=(kk == nk - 1)),
                         reads=[lhs_rg(kk), rhs_rg_fn(kk, c0, n)], writes=[prg(b)])
            return banks

        def rmsnorm(gc, W, cgs, squares_done=False):
            for i in range(0 if not squares_done else NCH, NCH):
                if i % 2 == 0:
                    S.op('act', lambda e, i=i: e.activation(
                        out=XN.ap(i * TW, i * TW + W), in_=H.ap(i * TW, i * TW + W), func=AF.Square),
                        reads=[H.rg(i * TW, i * TW + W)], writes=[XN.rg(i * TW, i * TW + W)])
                else:
                    S.op('dve', lambda e, i=i: e.tensor_tensor(
                        out=XN.ap(i * TW, i * TW + W), in0=H.ap(i * TW, i * TW + W),
                        in1=H.ap(i * TW, i * TW + W), op=ALU.mult),
                        reads=[H.rg(i * TW, i * TW + W)], writes=[XN.rg(i * TW, i * TW + W)])
            banks = mm_group(cgs, NCH,
                             lambda kk: ONES.ap(0, 128),
                             lambda kk, c0, n: XN.ap(kk * TW + c0, kk * TW + c0 + n),
                             lambda kk: ONES.rg(0, 128),
                             lambda kk, c0, n: XN.rg(kk * TW + c0, kk * TW + c0 + n))
            for (c0, n), b in zip(cgs, banks):
                S.op('act', lambda e, c0=c0, n=n, b=b: e.activation(
                    out=RS.ap(c0, c0 + n), in_=pap(b, 0, n), func=AF.Ln,
                    scale=1.0 / D, bias=EPS),
                    reads=[prg(b)], writes=[RS.rg(c0, c0 + n)])
            S.op('act', lambda e: e.activation(out=RS.ap(TW, TW + W), in_=RS.ap(0, W), func=AF.Exp, scale=-0.5),
                 reads=[RS.rg(0, W)], writes=[RS.rg(TW, TW + W)])
            for i in range(NCH):
                S.op('dve', lambda e, i=i: e.scalar_tensor_tensor(
                    out=XN.ap(i * TW, i * TW + W), in0=H.ap(i * TW, i * TW + W),
                    scalar=gcol(gc + i), in1=RS.ap(TW, TW + W), op0=ALU.mult, op1=ALU.mult),
                    reads=[H.rg(i * TW, i * TW + W), RS.rg(TW, TW + W), GV.rg(gc + i, gc + i + 1)],
                    writes=[XN.rg(i * TW, i * TW + W)])

        sg_ctr = [0]

        def early_square(W):
            def f(i):
                S.op('act', lambda e: e.activation(
                    out=XN.ap(i * TW, i * TW + W), in_=H.ap(i * TW, i * TW + W), func=AF.Square),
                    reads=[H.rg(i * TW, i * TW + W)], writes=[XN.rg(i * TW, i * TW + W)])
            return f

        def ffn(ls, W, cgs, on_final=None, squares_done=False):
            rmsnorm(GC_FFN + ls * 16, W, cgs, squares_done)
            for q in range(NQ):
                for (f0, n) in ((QCH * q, 4), (QCH * q + 4, 4), (QCH * q + 8, 3)):
                    gb, gk, gw = ws_next('g')
                    ub, uk, uw = ws_next('u')
                    for jj in range(n):
                        a = f0 + jj - QCH * q
                        bg = mm_group(cgs, NCH,
                                      lambda kk: WR.ap(gb + kk * gw + jj * 128, gb + kk * gw + jj * 128 + 128),
                                      lambda kk, c0, nn: XN.ap(kk * TW + c0, kk * TW + c0 + nn),
                                      lambda kk: WR.rg(gb, gb + SLOT),
                                      lambda kk, c0, nn: XN.rg(kk * TW + c0, kk * TW + c0 + nn))
                        bu = mm_group(cgs, NCH,
                                      lambda kk: WR.ap(ub + kk * uw + jj * 128, ub + kk * uw + jj * 128 + 128),
                                      lambda kk, c0, nn: XN.ap(kk * TW + c0, kk * TW + c0 + nn),
                                      lambda kk: WR.rg(ub, ub + SLOT),
                                      lambda kk, c0, nn: XN.rg(kk * TW + c0, kk * TW + c0 + nn))
                        for (c0, nn), b1, b2 in zip(cgs, bg, bu):
                            so = (sg_ctr[0] % 2) * 512
                            sg_ctr[0] += 1
                            S.op('act', lambda e, so=so, nn=nn, b1=b1: e.activation(
                                out=SG.ap(so, so + nn), in_=pap(b1, 0, nn), func=AF.Silu),
                                reads=[prg(b1)], writes=[SG.rg(so, so + nn)])
                            S.op('dve', lambda e, so=so, c0=c0, nn=nn, b2=b2: e.tensor_tensor(
                                out=ACTB.ap(a * TW + c0, a * TW + c0 + nn),
                                in0=SG.ap(so, so + nn), in1=pap(b2, 0, nn), op=ALU.mult),
                                reads=[SG.rg(so, so + nn), prg(b2)],
                                writes=[ACTB.rg(a * TW + c0, a * TW + c0 + nn)])
                    ws_release()
                for dg in range(4):
                    db, dk, dw = ws_next('d')
                    for jj in range(4):
                        i = dg * 4 + jj
                        bs = mm_group(cgs, QCH,
                                      lambda kk: WR.ap(db + kk * dw + jj * 128, db + kk * dw + jj * 128 + 128),
                                      lambda kk, c0, nn: ACTB.ap(kk * TW + c0, kk * TW + c0 + nn),
                                      lambda kk: WR.rg(db, db + SLOT),
                                      lambda kk, c0, nn: ACTB.rg(kk * TW + c0, kk * TW + c0 + nn))
                        for (c0, nn), b in zip(cgs, bs):
                            S.op('dve', lambda e, c0=c0, nn=nn, b=b: e.scalar_tensor_tensor(
                                out=H.ap(i * TW + c0, i * TW + c0 + nn), in0=pap(b, 0, nn), scalar=0.5,
                                in1=H.ap(i * TW + c0, i * TW + c0 + nn), op0=ALU.mult, op1=ALU.add),
                                reads=[prg(b), H.rg(i * TW + c0, i * TW + c0 + nn)],
                                writes=[H.rg(i * TW + c0, i * TW + c0 + nn)])
                        if q == NQ - 1 and on_final is not None:
                            on_final(i)
                    ws_release()

        def proj_residual(kind, src, W, cgs, on_final=None):
            for g in range(4):
                wb, wk, ww = ws_next(kind)
                for jj in range(4):
                    i = g * 4 + jj
                    bs = mm_group(cgs, NCH,
                                  lambda kk: WR.ap(wb + kk * ww + jj * 128, wb + kk * ww + jj * 128 + 128),
                                  lambda kk, c0, nn: src.ap(kk * TW + c0, kk * TW + c0 + nn),
                                  lambda kk: WR.rg(wb, wb + SLOT),
                                  lambda kk, c0, nn: src.rg(kk * TW + c0, kk * TW + c0 + nn))
                    for (c0, nn), b in zip(cgs, bs):
                        S.op('dve', lambda e, c0=c0, nn=nn, b=b: e.tensor_tensor(
                            out=H.ap(i * TW + c0, i * TW + c0 + nn), in0=pap(b, 0, nn),
                            in1=H.ap(i * TW + c0, i * TW + c0 + nn), op=ALU.add),
                            reads=[prg(b), H.rg(i * TW + c0, i * TW + c0 + nn)],
                            writes=[H.rg(i * TW + c0, i * TW + c0 + nn)])
                    if on_final is not None:
                        on_final(i)
                ws_release()

        sq_ctr = [0]

        def headnorm_a(b, n):
            sq = (sq_ctr[0] % 4) * 512
            sq_ctr[0] += 1
            S.op('act', lambda e: e.activation(out=SQB.ap(sq, sq + n), in_=pap(b, 0, n), func=AF.Square),
                 reads=[prg(b)], writes=[SQB.rg(sq, sq + n)])
            return sq

        def headnorm_b(b, sq, c0, n, gcolumn, out_fn, out_rg, sb=None):
            if sb is None:
                sb = bank()
            S.op('pe', lambda e: e.matmul(pap(sb, 0, n), lhsT=ONES.ap(0, 128), rhs=SQB.ap(sq, sq + n),
                                          start=True, stop=True),
                 reads=[ONES.rg(0, 128), SQB.rg(sq, sq + n)], writes=[prg(sb)])
            S.op('act', lambda e: e.activation(out=RS.ap(c0, c0 + n), in_=pap(sb, 0, n), func=AF.Ln,
                                               scale=1.0 / 128, bias=EPS),
                 reads=[prg(sb)], writes=[RS.rg(c0, c0 + n)])
            S.op('act', lambda e: e.activation(out=RS.ap(TW + c0, TW + c0 + n), in_=RS.ap(c0, c0 + n),
                                               func=AF.Exp, scale=-0.5),
                 reads=[RS.rg(c0, c0 + n)], writes=[RS.rg(TW + c0, TW + c0 + n)])
            S.op('dve', lambda e: e.scalar_tensor_tensor(
                out=out_fn(), in0=pap(b, 0, n), scalar=gcol(gcolumn),
                in1=RS.ap(TW + c0, TW + c0 + n), op0=ALU.mult, op1=ALU.mult),
                reads=[prg(b), RS.rg(TW + c0, TW + c0 + n), GV.rg(gcolumn, gcolumn + 1)],
                writes=[out_rg])

        xT3 = xT.rearrange("(k p) t -> p k t", p=128)
        H3 = H.t[:, 0:NCH * TW0].rearrange("p (k w) -> p k w", w=TW0)
        hs3 = h_scr.rearrange("p (k w) -> p k w", w=1040)
        yT3 = yT.rearrange("(k p) t -> p k t", p=128)
        stg_ctr = [0]
        NM = 768

        def all_h(c0, c1):
            return [H.rg(i * TW + c0, i * TW + c1) for i in range(NCH)]

        def sgslot():
            so = (sg_ctr[0] % 2) * 512
            sg_ctr[0] += 1
            return so

        for t in range(2):
            ne = 18 if t == 0 else 0
            W = NM + ne
            cgs = [(0, 512), (512, W - 512)]
            tk0 = NM * t
            if t == 0:
                S.dma('sp', 'c_h', H3[:, :, 0:NM], xT3[:, :, tk0:tk0 + NM], writes=all_h(0, NM))
                S.dma('sp', 'c_hx', H3[:, :, NM:NM + 18], xT3[:, :, 1536:1554], writes=all_h(NM, NM + 18))
            mark('L0t%d ffn0' % t)
            ffn(0, W, cgs, on_final=early_square(W))
            mark('L0t%d conv' % t)
            rmsnorm(GC_MIX + 0, W, cgs, squares_done=True)
            for g in range(8):
                cb, _, cw = ws_next('incx')
                xb, xw = cb + 4096, cw
                bb, _, bw = ws_next('inb')
                for jj in range(2):
                    i = g * 2 + jj

                    def wl(base, w):
                        return (lambda kk: WR.ap(base + kk * w + jj * 128, base + kk * w + jj * 128 + 128),
                                lambda kk: WR.rg(base, base + SLOT))
                    rf = lambda kk, c0, nn: XN.ap(kk * TW + c0, kk * TW + c0 + nn)
                    rr = lambda kk, c0, nn: XN.rg(kk * TW + c0, kk * TW + c0 + nn)
                    l1, l2 = wl(cb, cw)
                    bc = mm_group(cgs, NCH, l1, rf, l2, rr)
                    l1, l2 = wl(xb, xw)
                    bx = mm_group(cgs, NCH, l1, rf, l2, rr)
                    l1, l2 = wl(bb, bw)
                    bbk = mm_group(cgs, NCH, l1, rf, l2, rr)
                    for (c0, nn), bC, bX in zip(cgs, bc, bx):
                        so = sgslot()
                        S.op('act', lambda e, so=so, nn=nn, bC=bC: e.activation(
                            out=SG.ap(so, so + nn), in_=pap(bC, 0, nn), func=AF.Copy),
                            reads=[prg(bC)], writes=[SG.rg(so, so + nn)])
                        nmain = min(c0 + nn, NM) - c0
                        S.op('dve', lambda e, so=so, c0=c0, nmain=nmain, bX=bX: e.tensor_tensor(
                            out=UB.ap(2 + c0, 2 + c0 + nmain), in0=SG.ap(so, so + nmain),
                            in1=pap(bX, 0, nmain), op=ALU.mult),
                            reads=[SG.rg(so, so + nmain), prg(bX)], writes=[UB.rg(2 + c0, 2 + c0 + nmain)])
                        if nn > nmain:
                            S.op('dve', lambda e, so=so, nmain=nmain, bX=bX: e.tensor_tensor(
                                out=UE.ap(0, 18), in0=SG.ap(so + nmain, so + nmain + 18),
                                in1=pap(bX, nmain, nmain + 18), op=ALU.mult),
                                reads=[SG.rg(so + nmain, so + nmain + 18), prg(bX)], writes=[UE.rg(0, 18)])
                            S.op('dve', lambda e: e.tensor_copy(out=UB.ap(0, 2), in_=UE.ap(16, 18)),
                                 reads=[UE.rg(16, 18)], writes=[UB.rg(0, 2)])
                            S.op('dve', lambda e: e.tensor_copy(out=US.ap(2, 18), in_=UE.ap(0, 16)),
                                 reads=[UE.rg(0, 16)], writes=[US.rg(2, 18)])
                            S.op('dve', lambda e: e.tensor_copy(out=US.ap(0, 2), in_=STT.ap(2 * i, 2 * i + 2)),
                                 reads=[STT.rg(2 * i, 2 * i + 2)], writes=[US.rg(0, 2)])
                            S.op('dve', lambda e: e.tensor_copy(out=OCS.ap(2 * i, 2 * i + 2), in_=US.ap(16, 18)),
                                 reads=[US.rg(16, 18)], writes=[OCS.rg(2 * i, 2 * i + 2)])
                    if not ne:
                        S.op('dve', lambda e: e.tensor_copy(out=UB.ap(0, 2), in_=UPREV.ap(2 * i, 2 * i + 2)),
                             reads=[UPREV.rg(2 * i, 2 * i + 2)], writes=[UB.rg(0, 2)])
                    S.op('dve', lambda e: e.tensor_copy(out=UPREV.ap(2 * i, 2 * i + 2), in_=UB.ap(NM, NM + 2)),
                         reads=[UB.rg(NM, NM + 2)], writes=[UPREV.rg(2 * i, 2 * i + 2)])
                    if t == 1:
                        S.op('dve', lambda e: e.tensor_copy(out=OCP.ap(2 * i, 2 * i + 2), in_=UB.ap(NM, NM + 2)),
                             reads=[UB.rg(NM, NM + 2)], writes=[OCP.rg(2 * i, 2 * i + 2)])

                    def conv(src, s0, n, tmp, bbank, bc0, out_lo):
                        w0 = gcol(GC_CONV + 0 * 16 + i)
                        w1 = gcol(GC_CONV + 1 * 16 + i)
                        w2 = gcol(GC_CONV + 2 * 16 + i)
                        gr = [GV.rg(GC_CONV, GC_CONV + 48)]
                        A = (0, n)
                        Bq = (tmp.n // 2, tmp.n // 2 + n)
                        S.op('dve', lambda e: e.tensor_scalar(
                            out=tmp.ap(*A), in0=src.ap(s0, s0 + n), scalar1=w0, scalar2=None, op0=ALU.mult),
                            reads=[src.rg(s0, s0 + n)] + gr, writes=[tmp.rg(*A)])
                        S.op('dve', lambda e: e.scalar_tensor_tensor(
                            out=tmp.ap(*Bq), in0=src.ap(s0 + 1, s0 + 1 + n), scalar=w1, in1=tmp.ap(*A),
                            op0=ALU.mult, op1=ALU.add),
                            reads=[src.rg(s0 + 1, s0 + 1 + n), tmp.rg(*A)] + gr, writes=[tmp.rg(*Bq)])
                        S.op('dve', lambda e: e.scalar_tensor_tensor(
                            out=tmp.ap(*A), in0=src.ap(s0 + 2, s0 + 2 + n), scalar=w2, in1=tmp.ap(*Bq),
                            op0=ALU.mult, op1=ALU.add),
                            reads=[src.rg(s0 + 2, s0 + 2 + n), tmp.rg(*Bq)] + gr, writes=[tmp.rg(*A)])
                        S.op('dve', lambda e: e.tensor_tensor(
                            out=ACTB.ap(out_lo, out_lo + n), in0=tmp.ap(*A), in1=pap(bbank, bc0, bc0 + n), op=ALU.mult),
                            reads=[tmp.rg(*A), prg(bbank)], writes=[ACTB.rg(out_lo, out_lo + n)])

                    conv(UB, 0, 512, T1, bbk[0], 0, i * TW)
                    conv(UB, 512, 256, T1, bbk[1], 0, i * TW + 512)
                    if ne:
                        conv(US, 0, 16, TS, bbk[1], 256, i * TW + NM)
                        S.op('dve', lambda e: e.tensor_copy(out=ACTB.ap(i * TW + NM + 16, i * TW + NM + 18),
                                                            in_=pap(bbk[1], 272, 274)),
                             reads=[prg(bbk[1])], writes=[ACTB.rg(i * TW + NM + 16, i * TW + NM + 18)])
                ws_release()
            mark('L0t%d wout' % t)
            proj_residual('out', ACTB, W, cgs, on_final=early_square(W))
            mark('L0t%d ffn1' % t)
            ffn(1, W, cgs, on_final=early_square(W), squares_done=True)
            if t == 0:
                S.dma('sp', 'c_hs', hs3[:, :, 0:256], H3[:, :, 512:768], reads=all_h(512, 768),
                      writes=[(sp_hscr, i * 1040, i * 1040 + 256) for i in range(NCH)])
                S.dma('sp', 'c_hs2', hs3[:, :, 1024:1040], H3[:, :, NM:NM + 16], reads=all_h(NM, NM + 16),
                      writes=[(sp_hscr, i * 1040 + 1024, i * 1040 + 1040) for i in range(NCH)])
            else:
                S.dma('sp', 'c_hs', hs3[:, :, 256:1024], H3[:, :, 0:NM], reads=all_h(0, NM),
                      writes=[(sp_hscr, i * 1040 + 256, i * 1040 + 1024) for i in range(NCH)])
            mark('L0t%d kv' % t)
            rmsnorm(GC_KV, W, cgs, squares_done=True)
            if t == 0:
                S.dma('sp', 'c_h', H3[:, :, 0:NM], xT3[:, :, NM:2 * NM], writes=all_h(0, NM))
            else:
                def l1h(c0, c1):
                    return [H.rg(i * TW1 + c0, i * TW1 + c1) for i in range(NCH)]
                H3b = H.t[:, 0:NCH * TW1].rearrange("p (k w) -> p k w", w=TW1)
                S.dma('sp', 'c_h', H3b[:, :, 0:512], hs3[:, :, 0:512],
                      reads=[(sp_hscr, i * 1040, i * 1040 + 512) for i in range(NCH)], writes=l1h(0, 512))
                S.dma('sp', 'c_hx', H3b[:, :, 512:528], hs3[:, :, 1024:1040],
                      reads=[(sp_hscr, i * 1040 + 1024, i * 1040 + 1040) for i in range(NCH)], writes=l1h(512, 528))
            pending = [None]

            def k_post(hd, bs, sqs, t=t, cgs=cgs, tk0=tk0):
                for (c0, nn), b, sq in zip(cgs, bs, sqs):
                    st = (stg_ctr[0] % 2) * 512
                    sl = stg_ctr[0] % 2
                    stg_ctr[0] += 1
                    headnorm_b(b, sq, c0, nn, GC_KG, lambda: STG.ap(st, st + nn), STG.rg(st, st + nn),
                               sb=6 + (0 if c0 == 0 else 1))
                    S.op('act', lambda e: e.activation(out=STGB.ap(st, st + nn), in_=STG.ap(st, st + nn), func=AF.Copy),
                         reads=[STG.rg(st, st + nn)], writes=[STGB.rg(st, st + nn)])
                    nmain = min(c0 + nn, NM) - c0
                    r0 = tk0 + c0
                    S.dma('sp', 'c_stgb%d' % sl, kt_scr[hd, :, r0:r0 + nmain], STGB.ap(st, st + nmain),
                          reads=[STGB.rg(st, st + nmain)], writes=[(sp_kt, hd * 1536 + r0, hd * 1536 + r0 + nmain)])
                    if t == 1:
                        lo, hi = max(c0, 256), c0 + nmain
                        if hi > lo:
                            S.dma('sp', 'c_stg%d' % sl, kpT[hd * 128:(hd + 1) * 128, lo - 256:hi - 256],
                                  STG.ap(st + lo - c0, st + hi - c0), reads=[STG.rg(st + lo - c0, st + hi - c0)])
                    if nn > nmain:
                        S.op('act', lambda e: e.activation(out=KST.ap(hd * 16, hd * 16 + 16),
                                                           in_=STG.ap(st + nmain, st + nmain + 16), func=AF.Copy),
                             reads=[STG.rg(st + nmain, st + nmain + 16)], writes=[KST.rg(hd * 16, hd * 16 + 16)])
                        S.dma('sp', 'c_stg%d' % sl, ksT[hd * 128:(hd + 1) * 128, :], STG.ap(st + nmain, st + nmain + 16),
                              reads=[STG.rg(st + nmain, st + nmain + 16)])

            for g in range(4):
                kb_, _, kw = ws_next('k')
                for jj in range(4):
                    hd = g * 4 + jj
                    bs = mm_group(cgs, NCH,
                                  lambda kk: WR.ap(kb_ + kk * kw + jj * 128, kb_ + kk * kw + jj * 128 + 128),
                                  lambda kk, c0, nn: XN.ap(kk * TW + c0, kk * TW + c0 + nn),
                                  lambda kk: WR.rg(kb_, kb_ + SLOT),
                                  lambda kk, c0, nn: XN.rg(kk * TW + c0, kk * TW + c0 + nn),
                                  banks=[2 * (hd % 3), 2 * (hd % 3) + 1][:len(cgs)])
                    sqs = [headnorm_a(b, nn) for (c0, nn), b in zip(cgs, bs)]
                    if pending[0] is not None:
                        k_post(*pending[0])
                    pending[0] = (hd, bs, sqs)
                ws_release()
            k_post(*pending[0])
            for g in range(4):
                vb_, _, vw = ws_next('v')
                blocks = [(tb * 128, 128) for tb in range(NM // 128)] + ([(NM, 16)] if ne else [])
                for (c0, nt) in blocks:
                    b = bank()
                    for kk in range(NCH):
                        S.op('pe', lambda e, kk=kk: e.matmul(
                            pap(b, 0, 512, 0, nt), lhsT=XN.ap(kk * TW + c0, kk * TW + c0 + nt),
                            rhs=WR.ap(vb_ + kk * vw, vb_ + kk * vw + 512), start=(kk == 0), stop=(kk == NCH - 1)),
                            reads=[XN.rg(kk * TW + c0, kk * TW + c0 + nt), WR.rg(vb_, vb_ + SLOT)], writes=[prg(b)])
                    st = (stg_ctr[0] % 2) * 512
                    sl = stg_ctr[0] % 2
                    stg_ctr[0] += 1
                    if nt == 128:
                        S.op('act', lambda e: e.activation(out=STGB.ap(st, st + 512), in_=pap(b, 0, 512), func=AF.Copy),
                             reads=[prg(b)], writes=[STGB.rg(st, st + 512)])
                        r0 = tk0 + c0
                        S.dma('sp', 'c_stgb%d' % sl, v_scr[r0:r0 + 128, g * 512:(g + 1) * 512], STGB.ap(st, st + 512),
                              reads=[STGB.rg(st, st + 512)], writes=[(sp_v, r0, r0 + 128)])
                        if t == 1 and c0 >= 256:
                            S.op('dve', lambda e: e.tensor_copy(out=STG.ap(st, st + 512), in_=pap(b, 0, 512)),
                                 reads=[prg(b)], writes=[STG.rg(st, st + 512)])
                            S.dma('sp', 'c_stg%d' % sl, vp[c0 - 256:c0 - 256 + 128, g * 512:(g + 1) * 512],
                                  STG.ap(st, st + 512), reads=[STG.rg(st, st + 512)])
                    else:
                        S.op('act', lambda e: e.activation(out=VS.ap(g * 512, g * 512 + 512, 0, 16),
                                                           in_=pap(b, 0, 512, 0, 16), func=AF.Copy),
                             reads=[prg(b)], writes=[VS.rg(g * 512, g * 512 + 512)])
                        S.op('dve', lambda e: e.tensor_copy(out=STG.ap(st, st + 512, 0, 16), in_=pap(b, 0, 512, 0, 16)),
                             reads=[prg(b)], writes=[STG.rg(st, st + 512)])
                        S.dma('sp', 'c_stg%d' % sl, vs[:, g * 512:(g + 1) * 512], STG.ap(st, st + 512, 0, 16),
                              reads=[STG.rg(st, st + 512)])
                ws_release()
            if t == 1:
                S.dma('sp', 'c_oc', convp, OCP.ap(0, 32), reads=[OCP.rg(0, 32)])
                S.dma('sp', 'c_oc', convs, OCS.ap(0, 32), reads=[OCS.rg(0, 32)])

        TW = TW1
        H3 = H.t[:, 0:NCH * TW1].rearrange("p (k w) -> p k w", w=TW1)
        QT = ACTB
        ATT = XN
        S.dma('sp', 'c_mask', MASK.ap(0, 640), maskT, writes=[MASK.rg(0, 640)])
        for t in range(2):
            ne = 16 if t == 0 else 0
            W = 512 + ne
            cgs = [(0, 384), (384, 144)] if ne else [(0, 512)]
            O = 512 + 512 * t
            if t == 1:
                S.dma('sp', 'c_h', H3[:, :, 0:512], hs3[:, :, 512:1024],
                      reads=[(sp_hscr, i * 1040 + 512, i * 1040 + 1024) for i in range(NCH)],
                      writes=all_h(0, 512))
            mark('L1t%d ffn0' % t)
            ffn(2, W, cgs, on_final=early_square(W))
            mark('L1t%d q' % t)
            rmsnorm(GC_MIX + 16, W, cgs, squares_done=True)
            pending = [None]

            def q_post(hd, bs, sqs, cgs=cgs):
                for (c0, nn), b, sq in zip(cgs, bs, sqs):
                    headnorm_b(b, sq, c0, nn, GC_QG,
                               lambda c0=c0, nn=nn: QT.ap(hd * TW + c0, hd * TW + c0 + nn),
                               QT.rg(hd * TW + c0, hd * TW + c0 + nn), sb=6 + (0 if c0 == 0 else 1))

            for g in range(4):
                qb_, _, qw = ws_next('q')
                for jj in range(4):
                    hd = g * 4 + jj
                    bs = mm_group(cgs, NCH,
                                  lambda kk: WR.ap(qb_ + kk * qw + jj * 128, qb_ + kk * qw + jj * 128 + 128),
                                  lambda kk, c0, nn: XN.ap(kk * TW + c0, kk * TW + c0 + nn),
                                  lambda kk: WR.rg(qb_, qb_ + SLOT),
                                  lambda kk, c0, nn: XN.rg(kk * TW + c0, kk * TW + c0 + nn),
                                  banks=[2 * (hd % 3), 2 * (hd % 3) + 1][:len(cgs)])
                    sqs = [headnorm_a(b, nn) for (c0, nn), b in zip(cgs, bs)]
                    if pending[0] is not None:
                        q_post(*pending[0])
                    pending[0] = (hd, bs, sqs)
                ws_release()
            q_post(*pending[0])
            mark('L1t%d attn' % t)
            def head_setup(hd, O=O, ne=ne):
                hb = hd % 2
                k0, v0, e0 = hb * 1024, hb * 1024, hb * 640
                S.dma('sp', 'c_kb%d' % hb, KB.ap(k0, k0 + 1024), kt_scr[hd, :, O - 512:O + 512],
                      reads=[(sp_kt, hd * 1536 + O - 512, hd * 1536 + O + 512)], writes=[KB.rg(k0, k0 + 1024)])
                S.dma('sp', 'c_vb%d' % hb,
                      VB.sl(v0, v0 + 1024).rearrange("p (b d) -> p b d", d=128),
                      v_scr[O - 512:O + 512, hd * 128:(hd + 1) * 128].rearrange("(b p) d -> p b d", p=128),
                      reads=[(sp_v, O - 512, O + 512)], writes=[VB.rg(v0, v0 + 1024)])
                S.dma('sp', 'c_eb%d' % hb, EB.ap(e0, e0 + 640), biasT[hd], writes=[EB.rg(e0, e0 + 640)])
                S.op('act', lambda e: e.activation(out=EB.ap(e0, e0 + 640), in_=EB.ap(e0, e0 + 640), func=AF.Exp),
                     reads=[EB.rg(e0, e0 + 640)], writes=[EB.rg(e0, e0 + 640)])
                S.op('dve', lambda e: e.tensor_tensor(out=EB.ap(e0, e0 + 640), in0=EB.ap(e0, e0 + 640),
                                                      in1=MASK.ap(0, 640), op=ALU.mult),
                     reads=[EB.rg(e0, e0 + 640), MASK.rg(0, 640)], writes=[EB.rg(e0, e0 + 640)])
                if ne:
                    c0_, s0_ = hb * 512, hb * 80
                    S.dma('pool', 'c_kc%d' % hb, KC.ap(c0_, c0_ + 512), kcT[hd], writes=[KC.rg(c0_, c0_ + 512)])
                    S.dma('pool', 'c_vc%d' % hb,
                          VC.sl(c0_, c0_ + 512).rearrange("p (b d) -> p b d", d=128),
                          vc[:, hd * 128:(hd + 1) * 128].rearrange("(b p) d -> p b d", p=128),
                          writes=[VC.rg(c0_, c0_ + 512)])
                    S.dma('sp', 'c_ebs%d' % hb, EBS.ap(s0_, s0_ + 80), biasTs[hd], writes=[EBS.rg(s0_, s0_ + 80)])
                    S.op('act', lambda e: e.activation(out=EBS.ap(s0_, s0_ + 80), in_=EBS.ap(s0_, s0_ + 80), func=AF.Exp),
                         reads=[EBS.rg(s0_, s0_ + 80)], writes=[EBS.rg(s0_, s0_ + 80)])

            def unit_qk(hd, qb):
                if qb == 0:
                    head_setup(hd)
                hb = hd % 2
                k0 = hb * 1024
                sl3 = (hd * 4 + qb) % 3
                sA, sB = 2 * sl3, 2 * sl3 + 1
                qlo = hd * TW + qb * 128
                for jj in range(5):
                    kb = qb + jj
                    ob, oc = (sA, jj * 128) if jj < 4 else (sB, 0)
                    S.op('pe', lambda e, kb=kb, ob=ob, oc=oc: e.matmul(
                        pap(ob, oc, oc + 128), lhsT=KB.ap(k0 + kb * 128, k0 + kb * 128 + 128),
                        rhs=QT.ap(qlo, qlo + 128), start=True, stop=True),
                        reads=[KB.rg(k0 + kb * 128, k0 + kb * 128 + 128), QT.rg(qlo, qlo + 128)],
                        writes=[prg(ob)])

            def unit_soft(hd, qb):
                hb = hd % 2
                e0 = hb * 640
                par = qb % 2
                sl3 = (hd * 4 + qb) % 3
                sA, sB = 2 * sl3, 2 * sl3 + 1
                eo, po = par * 640, par * 640
                S.op('act', lambda e: e.activation(out=E.ap(eo, eo + 640), in_=pap(sA, 0, 640), func=AF.Exp, scale=SCALE),
                     reads=[prg(sA), prg(sB)], writes=[E.rg(eo, eo + 640)])
                S.op('dve', lambda e: e.tensor_tensor(out=PM.ap(po, po + 640), in0=E.ap(eo, eo + 640),
                                                      in1=EB.ap(e0, e0 + 640), op=ALU.mult),
                     reads=[E.rg(eo, eo + 640), EB.rg(e0, e0 + 640)], writes=[PM.rg(po, po + 640)])

            def head_release(hd):
                S.op('dve', lambda e: e.tensor_copy(out=T1.ap(0, 512), in_=pap(6, 0, 512)),
                     reads=[prg(6)], writes=[T1.rg(0, 512)])
                S.op('act', lambda e: e.activation(out=RDEN.ap(0, 512), in_=pap(7, 0, 512), func=AF.Ln),
                     reads=[prg(7)], writes=[RDEN.rg(0, 512)])

            def head_finish(hd):
                S.op('act', lambda e: e.activation(out=RDEN.ap(0, 512), in_=RDEN.ap(0, 512), func=AF.Exp, scale=-1.0),
                     reads=[RDEN.rg(0, 512)], writes=[RDEN.rg(0, 512)])
                S.op('dve', lambda e: e.tensor_tensor(out=ATT.ap(hd * TW, hd * TW + 512), in0=T1.ap(0, 512),
                                                      in1=RDEN.ap(0, 512), op=ALU.mult),
                     reads=[T1.rg(0, 512), RDEN.rg(0, 512)], writes=[ATT.rg(hd * TW, hd * TW + 512)])

            def unit_pv(hd, qb, O=O, ne=ne):
                hb = hd % 2
                v0 = hb * 1024
                par = qb % 2
                po = par * 640
                atb = 6
                denb = 7
                for jj in range(5):
                    kb = qb + jj
                    S.op('pe', lambda e, kb=kb, jj=jj: e.matmul(
                        pap(atb, qb * 128, qb * 128 + 128), lhsT=VB.ap(v0 + kb * 128, v0 + kb * 128 + 128),
                        rhs=PM.ap(po + jj * 128, po + jj * 128 + 128), start=(jj == 0), stop=(jj == 4)),
                        reads=[VB.rg(v0 + kb * 128, v0 + kb * 128 + 128), PM.rg(po + jj * 128, po + jj * 128 + 128)],
                        writes=[prg(atb)])
                for jj in range(5):
                    kbg = (O - 512) // 128 + qb + jj
                    S.op('pe', lambda e, kbg=kbg, jj=jj: e.matmul(
                        pap(denb, qb * 128, qb * 128 + 128), lhsT=VONES.ap(kbg * 128, kbg * 128 + 128),
                        rhs=PM.ap(po + jj * 128, po + jj * 128 + 128), start=(jj == 0), stop=(jj == 4)),
                        reads=[VONES.rg(kbg * 128, kbg * 128 + 128), PM.rg(po + jj * 128, po + jj * 128 + 128)],
                        writes=[prg(denb)])
                if qb != 3:
                    return
                if ne:
                    head_release(hd)
                    head_finish(hd)
                if ne:
                    c0_, s0_ = hb * 512, hb * 80
                    sS = 2 * ((hd * 4 + 3) % 3)
                    ats, dens = atb, denb
                    qlo = hd * TW + 512
                    for jj in range(4):
                        S.op('pe', lambda e, jj=jj: e.matmul(
                            pap(sS, jj * 16, jj * 16 + 16), lhsT=KC.ap(c0_ + jj * 128, c0_ + jj * 128 + 128),
                            rhs=QT.ap(qlo, qlo + 16), start=True, stop=True),
                            reads=[KC.rg(c0_ + jj * 128, c0_ + jj * 128 + 128), QT.rg(qlo, qlo + 16)],
                            writes=[prg(sS)])
                    S.op('pe', lambda e: e.matmul(
                        pap(sS, 64, 80, 0, 16), lhsT=KST.ap(hd * 16, hd * 16 + 16),
                        rhs=QT.ap(qlo, qlo + 16), start=True, stop=True),
                        reads=[KST.rg(hd * 16, hd * 16 + 16), QT.rg(qlo, qlo + 16)], writes=[prg(sS)])
                    S.op('act', lambda e: e.activation(out=ESM.ap(0, 64), in_=pap(sS, 0, 64), func=AF.Exp, scale=SCALE),
                         reads=[prg(sS)], writes=[ESM.rg(0, 64)])
                    S.op('act', lambda e: e.activation(out=ESM.ap(64, 80, 0, 16), in_=pap(sS, 64, 80, 0, 16), func=AF.Exp, scale=SCALE),
                         reads=[prg(sS)], writes=[ESM.rg(64, 80)])
                    S.op('dve', lambda e: e.tensor_tensor(out=PSM.ap(0, 64), in0=ESM.ap(0, 64), in1=EBS.ap(s0_, s0_ + 64), op=ALU.mult),
                         reads=[ESM.rg(0, 64), EBS.rg(s0_, s0_ + 64)], writes=[PSM.rg(0, 64)])
                    S.op('dve', lambda e: e.tensor_tensor(out=PSM.ap(64, 80, 0, 16), in0=ESM.ap(64, 80, 0, 16),
                                                          in1=EBS.ap(s0_ + 64, s0_ + 80, 0, 16), op=ALU.mult),
                         reads=[ESM.rg(64, 80), EBS.rg(s0_ + 64, s0_ + 80)], writes=[PSM.rg(64, 80)])
                    for (ob, lfn, lrg) in ((ats, lambda jj: VC.ap(c0_ + jj * 128, c0_ + jj * 128 + 128),
                                            lambda jj: VC.rg(c0_ + jj * 128, c0_ + jj * 128 + 128)),
                                           (dens, lambda jj: ONES.ap(0, 128), lambda jj: ONES.rg(0, 128))):
                        for jj in range(4):
                            S.op('pe', lambda e, jj=jj, ob=ob, lfn=lfn: e.matmul(
                                pap(ob, 0, 16), lhsT=lfn(jj), rhs=PSM.ap(jj * 16, jj * 16 + 16),
                                start=(jj == 0), stop=False),
                                reads=[lrg(jj), PSM.rg(jj * 16, jj * 16 + 16)], writes=[prg(ob)])
                        if ob == ats:
                            l5, l5r = VS.ap(hd * 128, hd * 128 + 128, 0, 16), VS.rg(hd * 128, hd * 128 + 128)
                        else:
                            l5, l5r = ONES.ap(0, 128, 0, 16), ONES.rg(0, 128)
                        S.op('pe', lambda e, ob=ob, l5=l5: e.matmul(
                            pap(ob, 0, 16), lhsT=l5, rhs=PSM.ap(64, 80, 0, 16), start=False, stop=True),
                            reads=[l5r, PSM.rg(64, 80)], writes=[prg(ob)])
                    S.op('act', lambda e: e.activation(out=RDENS.ap(0, 16), in_=pap(dens, 0, 16), func=AF.Ln),
                         reads=[prg(dens)], writes=[RDENS.rg(0, 16)])
                    S.op('act', lambda e: e.activation(out=RDENS.ap(0, 16), in_=RDENS.ap(0, 16), func=AF.Exp, scale=-1.0),
                         reads=[RDENS.rg(0, 16)], writes=[RDENS.rg(0, 16)])
                    S.op('dve', lambda e: e.tensor_tensor(out=ATT.ap(hd * TW + 512, hd * TW + 528), in0=pap(ats, 0, 16),
                                                          in1=RDENS.ap(0, 16), op=ALU.mult),
                         reads=[prg(ats), RDENS.rg(0, 16)], writes=[ATT.rg(hd * TW + 512, hd * TW + 528)])

            units = [(hd, qb) for hd in range(16) for qb in range(4)]
            unit_qk(*units[0])
            unit_qk(*units[1])
            prev = None
            for ui, u in enumerate(units):
                unit_soft(*u)
                if ui + 2 < len(units):
                    unit_qk(*units[ui + 2])
                if prev is not None:
                    head_release(prev)
                unit_pv(*u)
                if prev is not None:
                    head_finish(prev)
                    prev = None
                if u[1] == 3 and not ne:
                    prev = u[0]
            if prev is not None:
                head_release(prev)
                head_finish(prev)
            bank_ctr[0] = 0
            mark('L1t%d wo' % t)
            proj_residual('o', ATT, W, cgs)
            mark('L1t%d ffn1' % t)
            def y_store(i, t=t, ne=ne):
                S.dma('sp', 'c_y', yT[i * 128:(i + 1) * 128, 512 * t:512 * t + 512], H.ap(i * TW, i * TW + 512),
                      reads=[H.rg(i * TW, i * TW + 512)])
                if ne:
                    S.dma('sp', 'c_y2', yT[i * 128:(i + 1) * 128, 1024:1040], H.ap(i * TW + 512, i * TW + 528),
                          reads=[H.rg(i * TW + 512, i * TW + 528)])
            ffn(3, W, cgs, on_final=y_store)

        mark('end')
        assert ws['next'] == len(plan), (ws['next'], len(plan))
        S.final_wait('sp')
    return nc


def _vec16(v):
    return np.ascontiguousarray(np.asarray(v, np.float32).reshape(16, 128).T)


_NC_CACHE = {}


def kernel(x_prompt, x_sample, state_conv, cache_k, cache_v, ffn_norm, ffn_w_gate, ffn_w_up,
           ffn_w_down, mix_norm, conv_w_in, conv_w, conv_w_out, kv_norm, w_kv, k_gain,
           w_q, q_gain, rel_bias, w_o):
    f32 = np.float32
    x_prompt = np.asarray(x_prompt, f32)
    x_sample = np.asarray(x_sample, f32)
    state_conv = np.asarray(state_conv, f32)
    cache_k = np.asarray(cache_k, f32)
    cache_v = np.asarray(cache_v, f32)
    rel_bias = np.asarray(rel_bias, f32)

    gv = np.zeros((128, NGV), f32)
    fn = np.asarray(ffn_norm, f32)
    for l in range(2):
        for s in range(2):
            gv[:, GC_FFN + (l * 2 + s) * 16: GC_FFN + (l * 2 + s) * 16 + 16] = _vec16(fn[l, s])
        gv[:, GC_MIX + l * 16: GC_MIX + l * 16 + 16] = _vec16(np.asarray(mix_norm, f32)[l])
    gv[:, GC_KV:GC_KV + 16] = _vec16(kv_norm)
    cw = np.asarray(conv_w, f32)[0]
    for tap in range(3):
        gv[:, GC_CONV + tap * 16: GC_CONV + tap * 16 + 16] = _vec16(cw[tap])
    gv[:, GC_KG] = np.asarray(k_gain, f32)
    gv[:, GC_QG] = np.asarray(q_gain, f32)[0]

    rb = rel_bias[0]
    kk = np.arange(128)[:, None, None]
    jj = np.arange(5)[None, :, None]
    ql = np.arange(128)[None, None, :]
    dist = (4 - jj) * 128 + ql - kk
    idx = np.clip(dist, -128, 128) + 128
    biasT = np.ascontiguousarray(rb[:, idx].reshape(16, 128, 640))
    mask = np.ones((128, 5, 128), f32)
    inval0 = (kk < 64) & (ql >= 64)
    inval4 = (kk >= 64) & (ql < 64)
    mask[:, 0:1, :][np.broadcast_to(inval0, (128, 1, 128))] = 0.0
    mask[:, 4:5, :][np.broadcast_to(inval4, (128, 1, 128))] = 0.0
    maskT = np.ascontiguousarray(mask.reshape(128, 640))
    kidx = (np.arange(5)[None, :, None] * 128 + np.arange(128)[:, None, None])
    q = np.arange(16)[None, None, :]
    dist_s = q + 512 - kidx
    idx_s = np.clip(dist_s, -128, 128) + 128
    bts = rb[:, idx_s]
    valid_s = np.broadcast_to(kidx < 528, idx_s.shape)
    bts = np.where(valid_s[None], bts, 0.0).astype(f32)
    biasTs = np.ascontiguousarray(bts.reshape(16, 128, 80))

    wg = np.ascontiguousarray(np.asarray(ffn_w_gate, f32).reshape(4, D, DFF))
    wu = np.ascontiguousarray(np.asarray(ffn_w_up, f32).reshape(4, D, DFF))
    wd = np.ascontiguousarray(np.asarray(ffn_w_down, f32).reshape(4, DFF, D))
    shared = {
        "w_gate": wg, "w_up": wu, "w_down": wd,
        "w_in": np.ascontiguousarray(np.asarray(conv_w_in, f32)[0]),
        "w_out": np.ascontiguousarray(np.asarray(conv_w_out, f32)[0]),
        "w_kv": np.ascontiguousarray(np.asarray(w_kv, f32)),
        "w_q": np.ascontiguousarray(np.asarray(w_q, f32)[0]),
        "w_o": np.ascontiguousarray(np.asarray(w_o, f32)[0]),
        "gv": gv, "biasT": biasT, "biasTs": biasTs, "maskT": maskT,
    }

    in_maps = []
    for c in range(8):
        b, s = c // 4, c % 4
        start = 1024 * s
        xs = np.zeros((NTOK0, D), f32)
        if s > 0:
            xs[0:512] = x_prompt[b, start - 512:start]
            xs[1552:1554] = x_prompt[b, start - 514:start - 512]
        xs[512:1536] = x_prompt[b, start:start + 1024]
        xs[1536:1552] = x_sample[c]
        vones = np.ones((128, 12, 128), f32)
        if s == 0:
            vones[:, 0:4, :] = 0.0
        m = dict(shared)
        m["xT"] = np.ascontiguousarray(xs.T)
        m["stT"] = np.ascontiguousarray(
            state_conv[0, c].T.reshape(16, 128, 2).transpose(1, 0, 2).reshape(128, 32))
        m["kcT"] = np.ascontiguousarray(cache_k[c].transpose(1, 2, 0))
        m["vc"] = np.ascontiguousarray(cache_v[c].reshape(512, D))
        m["vones"] = np.ascontiguousarray(vones.reshape(128, 1536))
        in_maps.append(m)

    if "nc" not in _NC_CACHE:
        _NC_CACHE["nc"] = build_nc()
    nc = _NC_CACHE["nc"]
    res = run_bass_kernel_spmd(nc, in_maps, core_ids=list(range(8)))
    R = res.results

    y_prompt = np.zeros((2, 4096, D), f32)
    y_sample = np.zeros((8, 16, D), f32)
    conv_p = np.zeros((1, 2, 2, D), f32)
    k_p = np.zeros((2, 512, 16, 128), f32)
    v_p = np.zeros((2, 512, 16, 128), f32)
    conv_s = np.zeros((1, 8, 2, D), f32)
    k_s = np.zeros((8, 16, 16, 128), f32)
    v_s = np.zeros((8, 16, 16, 128), f32)

    def unvec(a):
        return a.reshape(128, 16, 2).transpose(2, 1, 0).reshape(2, D)

    for c in range(8):
        b, s = c // 4, c % 4
        r = R[c]
        yT = np.asarray(r["yT"])
        y_prompt[b, 1024 * s:1024 * s + 1024] = yT[:, :1024].T
        y_sample[c] = yT[:, 1024:1040].T
        conv_s[0, c] = unvec(np.asarray(r["convs"]))
        k_s[c] = np.asarray(r["ksT"]).T.reshape(16, 16, 128)
        v_s[c] = np.asarray(r["vs"]).reshape(16, 16, 128)
        if s == 3:
            conv_p[0, b] = unvec(np.asarray(r["convp"]))
            k_p[b] = np.asarray(r["kpT"]).T.reshape(512, 16, 128)
            v_p[b] = np.asarray(r["vp"]).reshape(512, 16, 128)
    return (y_prompt, y_sample, conv_p, k_p, v_p, conv_s, k_s, v_s)
```
